# Optimizing a Trainium2 kernel written in Bass

```python
import jax, jax.numpy as jnp
from jax import lax
import numpy as np

D_MODEL = 2048
BATCH = 4
SEQ = 2048
DEPTH = 1

CHUNK = 64
PLE_DIM = 256
EPS = 1e-6

MIX_A = D_MODEL // 2
CONV_GROUPS = 8
CONV_WIDTH = 3

MIX_B = D_MODEL // 2
GLA_HEADS = 4
GLA_DK_TOTAL = MIX_B // 2
GLA_DK = GLA_DK_TOTAL // GLA_HEADS
GLA_DV = MIX_B // GLA_HEADS
GLA_GATE_RANK = 16
GLA_TAU = 16.0

D_FF = ((8 * D_MODEL // 3 + 255) // 256) * 256

IN_SPLITS = (MIX_A, MIX_A, MIX_A,
             GLA_DK_TOTAL, GLA_DK_TOTAL, MIX_B, MIX_B, GLA_GATE_RANK,
             D_MODEL, D_MODEL)
IN_COLS = sum(IN_SPLITS)

kernel_name = "hybrid_shortconv_gla_gated_merge_block"


def rms_norm(x, gain):
    xf = x.astype(jnp.float32)
    y = xf * lax.rsqrt(jnp.mean(xf * xf, axis=-1, keepdims=True) + EPS)
    return (y * gain.astype(jnp.float32)).astype(x.dtype)


def short_conv_branch(u_x, u_b, u_c, conv_w, w_out):
    u = u_c * u_x
    y = lax.conv_general_dilated(
        u, conv_w[:, None, :].astype(u.dtype), window_strides=(1,),
        padding=[(CONV_WIDTH - 1, 0)],
        dimension_numbers=('NWC', 'WIO', 'NWC'),
        feature_group_count=MIX_A)
    return (u_b * y) @ w_out


def gla_branch(q, k, v, og, a_lr, w_alpha_up, b_alpha_up, head_gain, w_out):
    out_dtype = v.dtype
    bsz, seq = q.shape[0], q.shape[1]
    n = seq // CHUNK
    f32 = jnp.float32
    z = a_lr.astype(f32) @ w_alpha_up.astype(f32) + b_alpha_up.astype(f32)
    log_a = jax.nn.log_sigmoid(z) / GLA_TAU
    shp_k = (bsz, n, CHUNK, GLA_HEADS, GLA_DK)
    q = q.astype(f32).reshape(shp_k) * (GLA_DK ** -0.5)
    k = k.astype(f32).reshape(shp_k)
    v = v.astype(f32).reshape(bsz, n, CHUNK, GLA_HEADS, GLA_DV)
    b = jnp.cumsum(log_a.reshape(shp_k), axis=2)
    b_last = b[:, :, -1:]
    mid = b[:, :, CHUNK // 2:CHUNK // 2 + 1]
    a_fwd = jnp.einsum('bnthd,bnshd->bnhts', q * jnp.exp(b - mid), k * jnp.exp(mid - b))
    a_rev = jnp.einsum('bnthd,bnshd->bnhts', q * jnp.exp(mid - b), k * jnp.exp(b - mid))
    lower = jnp.tril(jnp.ones((CHUNK, CHUNK), dtype=bool))
    att = jnp.where(lower, a_fwd, a_rev)
    o_intra = jnp.einsum('bnhts,bnshv->bnthv', att, v)
    u_c = jnp.einsum('bnshd,bnshv->bnhdv', k * jnp.exp(b_last - b), v)
    decay = jnp.exp(b_last[:, :, 0])

    def step(state, inp):
        dec, upd = inp
        return dec[..., None] * state + upd, state

    s0 = jnp.zeros((bsz, GLA_HEADS, GLA_DK, GLA_DV), f32)
    _, s_in = lax.scan(step, s0, (jnp.moveaxis(decay, 1, 0), jnp.moveaxis(u_c, 1, 0)))
    s_in = jnp.moveaxis(s_in, 0, 1)
    o_inter = jnp.einsum('bnthd,bnhdv->bnthv', q * jnp.exp(b), s_in)
    o = (o_intra + o_inter).reshape(bsz, seq, GLA_HEADS, GLA_DV)
    o = o * lax.rsqrt(jnp.mean(o * o, axis=-1, keepdims=True) + EPS) * head_gain.astype(f32)
    o = o.reshape(bsz, seq, MIX_B) * jax.nn.silu(og.astype(f32))
    return o.astype(out_dtype) @ w_out


def setup_inputs(seed: int = 0) -> dict:
    key = jax.random.key(seed)
    ks = jax.random.split(key, 24)
    f32 = jnp.float32

    def nrm(k, shape, scale):
        return jax.random.normal(k, shape, f32) * scale

    def gain(k, dim):
        return 1.0 + 0.05 * jax.random.normal(k, (DEPTH, dim), f32)

    return {
        "x": nrm(ks[0], (BATCH, SEQ, D_MODEL), 1.0),
        "p": nrm(ks[1], (DEPTH, BATCH, SEQ, PLE_DIM), 1.0),
        "w_in": nrm(ks[2], (DEPTH, D_MODEL, IN_COLS), D_MODEL ** -0.5),
        "conv_w": nrm(ks[3], (DEPTH, CONV_WIDTH, MIX_A), CONV_WIDTH ** -0.5),
        "w_a_out": nrm(ks[4], (DEPTH, MIX_A, D_MODEL), MIX_A ** -0.5),
        "w_alpha_up": nrm(ks[5], (DEPTH, GLA_GATE_RANK, GLA_DK_TOTAL), GLA_GATE_RANK ** -0.5),
        "b_alpha_up": nrm(ks[6], (DEPTH, GLA_DK_TOTAL), 0.1),
        "gla_head_gain": gain(ks[7], GLA_DV),
        "w_b_out": nrm(ks[8], (DEPTH, MIX_B, D_MODEL), MIX_B ** -0.5),
        "w_mix_out": nrm(ks[9], (DEPTH, D_MODEL, D_MODEL), D_MODEL ** -0.5),
        "g_pre_mix": gain(ks[10], D_MODEL),
        "g_post_mix": gain(ks[11], D_MODEL),
        "g_pre_ffn": gain(ks[12], D_MODEL),
        "g_post_ffn": gain(ks[13], D_MODEL),
        "w_ff_gate": nrm(ks[14], (DEPTH, D_MODEL, D_FF), D_MODEL ** -0.5),
        "w_ff_up": nrm(ks[15], (DEPTH, D_MODEL, D_FF), D_MODEL ** -0.5),
        "w_ff_down": nrm(ks[16], (DEPTH, D_FF, D_MODEL), D_FF ** -0.5),
        "g_pre_ple": gain(ks[17], D_MODEL),
        "g_post_ple": gain(ks[18], D_MODEL),
        "w_ple_gate": nrm(ks[19], (DEPTH, D_MODEL, D_MODEL), D_MODEL ** -0.5),
        "w_ple_proj": nrm(ks[20], (DEPTH, PLE_DIM, D_MODEL), PLE_DIM ** -0.5),
    }


def reference(x, p, w_in, conv_w, w_a_out, w_alpha_up, b_alpha_up, gla_head_gain,
              w_b_out, w_mix_out, g_pre_mix, g_post_mix, g_pre_ffn, g_post_ffn,
              w_ff_gate, w_ff_up, w_ff_down, g_pre_ple, g_post_ple,
              w_ple_gate, w_ple_proj):
    split_idx = [int(c) for c in np.cumsum(IN_SPLITS)[:-1]]
    for i in range(DEPTH):
        h = rms_norm(x, g_pre_mix[i])
        proj = h @ w_in[i]
        (a_x, a_b, a_c, q, k, v, og, a_lr, gate_a, gate_b) = jnp.split(proj, split_idx, axis=-1)
        y_a = short_conv_branch(a_x, a_b, a_c, conv_w[i], w_a_out[i])
        y_b = gla_branch(q, k, v, og, a_lr, w_alpha_up[i], b_alpha_up[i],
                         gla_head_gain[i], w_b_out[i])
        mix = jax.nn.sigmoid(gate_a) * y_a + jax.nn.sigmoid(gate_b) * y_b
        x = x + rms_norm(mix @ w_mix_out[i], g_post_mix[i])
        h = rms_norm(x, g_pre_ffn[i])
        f = (jax.nn.silu(h @ w_ff_gate[i]) * (h @ w_ff_up[i])) @ w_ff_down[i]
        x = x + rms_norm(f, g_post_ffn[i])
        h = rms_norm(x, g_pre_ple[i])
        e = jax.nn.sigmoid(h @ w_ple_gate[i]) * (p[i] @ w_ple_proj[i])
        x = x + rms_norm(e, g_post_ple[i])
    return x
```

```python
import numpy as np
from contextlib import ExitStack
import ml_dtypes
import concourse.bass as bass
import concourse.mybir as mybir
from concourse.bass_utils import run_bass_kernel_spmd

F32 = mybir.dt.float32
BF16 = mybir.dt.bfloat16
AF = mybir.ActivationFunctionType
ALU = mybir.AluOpType

D = 2048
DFF = 5632
NTOK = 1024
TP = 512
EPS = 1e-6
C_AX, C_AB, C_AC, C_Q, C_K, C_V, C_OG, C_ALR, C_GA, C_GB = 0, 1024, 2048, 3072, 3584, 4096, 5120, 6144, 6160, 8208
INC = 10256
QSCALE = 128 ** -0.5


class Buf:
    __slots__ = ("name", "w", "r")

    def __init__(self, name):
        self.name = name
        self.w = {}
        self.r = {}


class Eng:
    def __init__(self, name, sem):
        self.name = name
        self.sem = sem
        self.count = 0
        self.waited = {}
        self.prog = []


class Prog:
    COMPUTE = ("pe", "act", "dve")

    def __init__(self, nc):
        self.nc = nc
        self.semcount = {}
        self.engs = {
            "pe": Eng("pe", "c_pe"),
            "act": Eng("act", "c_act"),
            "dve": Eng("dve", "c_dve"),
            "sp": Eng("sp", None),
            "pool": Eng("pool", None),
        }
        self.semnames = ["c_pe", "c_act", "c_dve"]

    @staticmethod
    def _flat(bs):
        out = []
        for b in bs:
            if isinstance(b, (list, tuple)):
                out.extend(Prog._flat(b))
            else:
                out.append(b)
        return out

    def _needs(self, E, reads, writes, partw):
        need = {}
        for b in reads:
            for s, v in b.w.items():
                if need.get(s, 0) < v:
                    need[s] = v
        for b in writes:
            for s, v in b.w.items():
                if need.get(s, 0) < v:
                    need[s] = v
            for s, v in b.r.items():
                if need.get(s, 0) < v:
                    need[s] = v
        for b in partw:
            for s, v in b.r.items():
                if need.get(s, 0) < v:
                    need[s] = v
        waits = []
        for s, v in need.items():
            if E.name == "pe" and s == "c_pe":
                continue
            if E.waited.get(s, 0) < v:
                waits.append((s, v))
                E.waited[s] = v
        return waits

    def _mark(self, ev, reads, writes, partw):
        for b in writes:
            b.w = {ev[0]: ev[1]}
            b.r = {}
        for b in partw:
            if b.w.get(ev[0], 0) < ev[1]:
                b.w[ev[0]] = ev[1]
        for b in reads:
            if b.r.get(ev[0], 0) < ev[1]:
                b.r[ev[0]] = ev[1]

    def emit(self, en, fn, reads=(), writes=(), partw=(), inc=True):
        E = self.engs[en]
        reads, writes, partw = self._flat(reads), self._flat(writes), self._flat(partw)
        if en != "pe":
            bank_r = [b for b in reads if b.name.startswith("bank")]
            if bank_r:
                reads = [b for b in reads if not b.name.startswith("bank")]
                writes = list(writes) + bank_r
        waits = self._needs(E, reads, writes, partw)
        if inc:
            E.count += 1
            ev = (E.sem, E.count)
            incs = [(E.sem, 1)]
        else:
            assert en == "pe"
            ev = (E.sem, E.count + 1)
            incs = []
        E.prog.append((waits, fn, incs))
        self._mark(ev, reads, writes, partw)

    def dma(self, en, sem, fn, reads=(), writes=(), partw=()):
        E = self.engs[en]
        if sem not in self.semcount:
            self.semcount[sem] = 0
            self.semnames.append(sem)
        reads, writes, partw = self._flat(reads), self._flat(writes), self._flat(partw)
        waits = self._needs(E, reads, writes, partw)
        self.semcount[sem] += 16
        ev = (sem, self.semcount[sem])
        E.prog.append((waits, fn, [(sem, 16)]))
        self._mark(ev, reads, writes, partw)

    def barrier(self, engines=("pe", "act", "dve", "sp")):
        for en in engines:
            E = self.engs[en]
            waits = []
            for fn_ in self.COMPUTE:
                F = self.engs[fn_]
                if F is E and en == "pe":
                    continue
                if E.waited.get(F.sem, 0) < F.count:
                    waits.append((F.sem, F.count))
                    E.waited[F.sem] = F.count
            if waits:
                E.prog.append((waits, None, []))

    def final_wait(self, en, sems):
        E = self.engs[en]
        waits = [(s, self.semcount[s]) for s in sems if self.semcount.get(s, 0) > 0]
        E.prog.append((waits, None, []))

    def replay(self):
        nc = self.nc
        with ExitStack() as st:
            H = {}
            for s in self.semnames:
                H[s] = st.enter_context(nc.semaphore(s))
            block = st.enter_context(nc.Block())

            def run(E):
                def body(e):
                    for waits, fn, incs in E.prog:
                        for s, v in waits:
                            e.wait_ge(H[s], v)
                        if fn is not None:
                            ins = fn(e)
                            for s, a in incs:
                                ins.then_inc(H[s], a)
                return body

            block.tensor(run(self.engs["pe"]))
            block.scalar(run(self.engs["act"]))
            block.vector(run(self.engs["dve"]))
            block.sync(run(self.engs["sp"]))
            block.gpsimd(run(self.engs["pool"]))


class Arena:
    def __init__(self, nc, nbytes):
        self.t = nc.alloc_sbuf_tensor("arena", [128, nbytes // 4], F32)
        self.nbytes = nbytes
        self.top = 0
        self.peak = 0

    def alloc(self, nbytes, dt=F32, pat=None, **kw):
        off = self.top
        self.top += (nbytes + 63) // 64 * 64
        assert self.top <= self.nbytes, f"arena overflow {self.top} > {self.nbytes}"
        self.peak = max(self.peak, self.top)
        v = self.t[:, off // 4:(off + nbytes) // 4]
        if dt != F32:
            v = v.bitcast(dt)
        if pat is not None:
            v = v.rearrange(pat, **kw)
        return v

    def mark(self):
        return self.top

    def reset(self, m):
        self.top = m


def build_program(n_pre_pass=2, n_own_pass=2, dbg=None, stop=None):
    nc = bass.Bass("TRN2", target_bir_lowering=False)
    P = Prog(nc)

    def din(name, shape, dt=F32):
        return nc.dram_tensor(name, list(shape), dt, kind="ExternalInput").ap()

    x_d = din("x", [NTOK, D])
    xp_d = din("xp", [NTOK, D])
    p_d = din("p", [NTOK, 256])
    w_in = din("w_in", [D, INC]).rearrange("(k p) c -> p k c", p=128)
    w_a_out = din("w_a_out", [1024, D]).rearrange("(k p) c -> p k c", p=128)
    w_b_out = din("w_b_out", [1024, D]).rearrange("(k p) c -> p k c", p=128)
    w_mix = din("w_mix_out", [D, D]).rearrange("(k p) c -> p k c", p=128)
    w_fg = din("w_ff_gate", [D, DFF]).rearrange("(k p) c -> p k c", p=128)
    w_fu = din("w_ff_up", [D, DFF]).rearrange("(k p) c -> p k c", p=128)
    w_fd = din("w_ff_down", [DFF, D]).rearrange("(k p) c -> p k c", p=128)
    w_pg = din("w_ple_gate", [D, D]).rearrange("(k p) c -> p k c", p=128)
    w_pp = din("w_ple_proj", [256, D]).rearrange("(k p) c -> p k c", p=128)
    gcol_d = din("gcol", [128, 48])
    gpost_d = din("gpost", [3, D])
    convw_d = din("convw", [128, 24])
    waug_d = din("waug", [33, 512])
    hg_d = din("hg", [1, 256])
    ident_d = din("ident", [128, 128], BF16)
    mfr_d = din("mfr", [128, 256], BF16)
    lm_d = din("lm", [128, 256])
    out_d = nc.dram_tensor("out", [NTOK, D], F32, kind="ExternalOutput").ap()
    dbg_out = {}
    if dbg:
        for name, (shape, dt) in dbg.items():
            dbg_out[name] = nc.dram_tensor("dbg_" + name, list(shape), dt, kind="ExternalOutput").ap()

    AR = Arena(nc, 206 * 1024)
    banks = [nc.alloc_psum_tensor(f"bank{i}", [128, 512], F32) for i in range(8)]
    bankbufs = [Buf(f"bank{i}") for i in range(8)]
    bank_i = [0]

    def nb():
        i = bank_i[0] % 8
        bank_i[0] += 1
        return banks[i], bankbufs[i]

    def dump_all(name, ap, bufs):
        if name in dbg_out:
            P.dma("sp", "dbg", lambda e, o=dbg_out[name], a=ap: e.dma_start(out=o, in_=a), reads=bufs)

    ident = AR.alloc(256, BF16)
    mfr = AR.alloc(512, BF16, "p (a b) -> p a b", a=2)
    lmm = AR.alloc(1024, F32, "p (a b) -> p a b", a=2)
    gcol = AR.alloc(192, F32, "p (a b) -> p a b", a=3)
    convw = AR.alloc(96, F32, "p (a b) -> p a b", a=8)
    hgb = AR.alloc(1024, F32)
    waug = AR.alloc(2048, F32)
    stats = AR.alloc(4 * 8 * 4, F32, "p (a b) -> p a b", a=4)
    ssqp = AR.alloc(32 * 4, F32)
    pst = AR.alloc(3 * 4 * 4, F32, "p (a b) -> p a b", a=3)
    pstb = Buf("pst")
    pT = AR.alloc(2 * 512 * 2, BF16, "p (k t) -> p k t", k=2)
    hstat = AR.alloc(4 * 16 * 4, F32, "p (a b) -> p a b", a=4)
    uh = AR.alloc(8 * 2 * 4, F32, "p (a b) -> p a b", a=8)
    dec = AR.alloc(4 * 8 * 4, F32, "p (a b) -> p a b", a=4)
    hTh = AR.alloc(16 * 2 * 2, BF16, "p (k t) -> p k t", k=16)
    hThb = Buf("hTh")
    X = AR.alloc(4 * D * 4, F32, "p (a b) -> p a b", a=4)
    S = AR.alloc(2 * 4 * 256 * 4, F32, "p (a h b) -> p a h b", a=2, h=4)
    NSLOT = 3
    slots = [AR.alloc(16384, BF16) for _ in range(NSLOT)]
    slotbufs = [Buf(f"ws{i}") for i in range(NSLOT)]
    slot_i = [0]
    BASE = AR.mark()

    cb = Buf("consts")
    xb = [Buf(f"x{i}") for i in range(4)]
    stb = [Buf(f"st{i}") for i in range(4)]
    hstb = [Buf(f"hst{i}") for i in range(4)]
    ssqpb = Buf("ssqp")
    uhb = Buf("uh")
    decb = [Buf(f"dec{h}") for h in range(4)]
    Sb = [[Buf(f"S{a}{h}") for h in range(4)] for a in range(2)]

    def wload(pieces):
        i = slot_i[0] % NSLOT
        slot_i[0] += 1
        flat, buf = slots[i], slotbufs[i]
        for j, (dstf, src) in enumerate(pieces):
            P.dma("pool", f"w{i}", lambda e, d=dstf(flat), s=src: e.dma_start(out=d, in_=s),
                  writes=[buf] if j == 0 else [], partw=[] if j == 0 else [buf])
        return flat, buf

    def v3(flat, k, c):
        return flat[:, 0:k * c].rearrange("p (k c) -> p k c", k=k)

    for dst, src in [(ident, ident_d), (mfr, mfr_d.rearrange("p (a b) -> p a b", a=2)),
                     (lmm, lm_d.rearrange("p (a b) -> p a b", a=2)),
                     (gcol, gcol_d.rearrange("p (a b) -> p a b", a=3)),
                     (convw, convw_d.rearrange("p (a b) -> p a b", a=8)),
                     (waug[0:33, :], waug_d)]:
        P.dma("sp", "ldc", lambda e, d=dst, s=src: e.dma_start(out=d, in_=s), partw=[cb])
    P.dma("sp", "ldc", lambda e: e.dma_start(out=hgb, in_=hg_d[0].partition_broadcast(128)), partw=[cb])
    for a in range(2):
        for h in range(4):
            P.emit("dve", lambda e, a=a, h=h: e.memset(S[:, a, h, :], 0.0), writes=[Sb[a][h]])
    P.emit("dve", lambda e: e.memset(uh, 0.0), writes=[uhb])

    def mm(out_ap, bankbuf, pairs, reads, new=True):
        n = len(pairs)
        for j, (l, r) in enumerate(pairs):
            P.emit("pe", lambda e, l=l, r=r, j=j: e.matmul(out_ap, lhsT=l, rhs=r, start=(j == 0), stop=(j == n - 1)),
                   reads=reads if j == 0 else (), writes=[bankbuf] if (j == 0 and new) else (),
                   partw=[bankbuf] if (j == 0 and not new) else (), inc=(j == n - 1))

    def mm_fm(bk, bankbuf, wv, hT, hTb, wb, nk, per_tile):
        if not per_tile:
            mm(bk[:, :], bankbuf, [(wv[:, k, :], hT[:, k, :]) for k in range(nk)], [wb, hTb])
            return
        for i in range(4):
            ts = slice(i * 128, (i + 1) * 128)
            mm(bk[:, ts], bankbuf, [(wv[:, k, :], hT[:, k, ts]) for k in range(nk)], [wb, hTb[i]], new=(i == 0))

    def rstd_ops(st_ap, sbuf, n):
        P.emit("act", lambda e: e.activation(out=st_ap[:, 1:2], in_=st_ap[:, 0:1], func=AF.Ln, scale=1.0 / n, bias=EPS),
               reads=[sbuf], writes=[sbuf])
        P.emit("act", lambda e: e.activation(out=st_ap[:, 2:3], in_=st_ap[:, 1:2], func=AF.Exp, scale=-0.5),
               reads=[sbuf], writes=[sbuf])

    def stage_norm(gi, hT, hTb):
        m0 = AR.mark()
        xn = [AR.alloc(4096, BF16) for _ in range(2)]
        xnb = [Buf("xn0"), Buf("xn1")]
        junk = AR.alloc(4096, BF16)
        jb = Buf("junk")
        for i in range(4):
            st = stats[:, i, :]
            P.emit("act", lambda e, i=i, st=st: e.activation(out=junk, in_=X[:, i, :], func=AF.Square, accum_out=st[:, 0:1]),
                   reads=[xb[i]], writes=[jb, stb[i]])
            rstd_ops(st, stb[i], D)
        pend = {}

        def do_xn(i):
            st = stats[:, i, :]
            P.emit("dve", lambda e, i=i, st=st: e.tensor_scalar(out=xn[i % 2], in0=X[:, i, :], scalar1=st[:, 2:3], scalar2=None, op0=ALU.mult),
                   reads=[xb[i], stb[i]], writes=[xnb[i % 2]])
            pend[i] = []
            for half in range(2):
                bk, bb = nb()
                ptv = bk[:, :].bitcast(BF16)
                for j in range(8):
                    kc = half * 8 + j
                    P.emit("pe", lambda e, i=i, kc=kc, j=j, ptv=ptv: e.transpose(out=ptv[:, j * 128:(j + 1) * 128], in_=xn[i % 2][:, kc * 128:(kc + 1) * 128], identity=ident),
                           reads=[xnb[i % 2], cb] if j == 0 else (), writes=[bb] if j == 0 else (), inc=(j == 7))
                pend[i].append((ptv, bb, half))

        def do_ev(i):
            for ptv, bb, half in pend[i]:
                P.emit("dve", lambda e, i=i, half=half, ptv=ptv: e.tensor_tensor(
                    out=hT[:, half * 8:(half + 1) * 8, i * 128:(i + 1) * 128], in0=ptv.rearrange("p (k t) -> p k t", k=8),
                    in1=gcol[:, gi, half * 8:(half + 1) * 8].unsqueeze(2).to_broadcast([128, 8, 128]), op=ALU.mult),
                    reads=[bb, cb], partw=[hTb[i]])

        do_xn(0)
        do_xn(1)
        do_ev(0)
        do_xn(2)
        do_ev(1)
        do_xn(3)
        do_ev(2)
        do_ev(3)
        AR.reset(m0)

    def load_x(src_d, row0):
        for i in range(4):
            P.dma("sp", f"ldx{i}", lambda e, i=i: e.dma_start(out=X[:, i, :], in_=src_d[row0 + i * 128: row0 + (i + 1) * 128, :]),
                  writes=[xb[i]])

    def post_norm_tile(i, m_sb, msb_bufs, nblk, last_store_row0=None, next_load=None):
        st = stats[:, i, :]
        P.emit("dve", lambda e: e.reduce_sum(out=st[:, 0:1], in_=ssqp[:, i * 8:i * 8 + nblk], axis=mybir.AxisListType.X),
               reads=[ssqpb], writes=[stb[i]])
        rstd_ops(st, stb[i], D)
        if last_store_row0 is None:
            P.emit("dve", lambda e: e.scalar_tensor_tensor(out=X[:, i, :], in0=m_sb[:, i, :], scalar=st[:, 2:3], in1=X[:, i, :], op0=ALU.mult, op1=ALU.add),
                   reads=[msb_bufs[i], stb[i], xb[i]], writes=[xb[i]])
        else:
            P.emit("dve", lambda e: e.scalar_tensor_tensor(out=m_sb[:, i, :], in0=m_sb[:, i, :], scalar=st[:, 2:3], in1=X[:, i, :], op0=ALU.mult, op1=ALU.add),
                   reads=[stb[i], xb[i]], writes=[msb_bufs[i]])
            P.dma("sp", f"stx{i}", lambda e: e.dma_start(out=out_d[last_store_row0 + i * 128: last_store_row0 + (i + 1) * 128, :], in_=m_sb[:, i, :]),
                  reads=[msb_bufs[i]])
            if next_load is not None:
                src_d, r0 = next_load
                P.dma("sp", f"ldx{i}", lambda e: e.dma_start(out=X[:, i, :], in_=src_d[r0 + i * 128: r0 + (i + 1) * 128, :]),
                      writes=[xb[i]])

    def load_gb(j):
        gb = AR.alloc(8192, F32)
        gbb = Buf("gb")
        P.dma("sp", "ldg", lambda e: e.dma_start(out=gb, in_=gpost_d[j].partition_broadcast(128)), writes=[gbb])
        return gb, gbb

    def gla_g0(hT, hTb, own):
        g = {}
        g["sp"] = AR.alloc(8192, F32, "p (i c) -> p i c", i=4)
        g["spb"] = [Buf(f"sp{i}") for i in range(4)]
        g["kdec"] = AR.alloc(4096, BF16, "p (i c) -> p i c", i=4)
        g["kdecb"] = [Buf(f"kdec{i}") for i in range(4)]
        g["v"] = AR.alloc(8192, BF16, "p (i c) -> p i c", i=4)
        g["vb"] = [Buf(f"v{i}") for i in range(4)]
        m0 = AR.mark()
        e4 = AR.alloc(8192, F32, "p (i c) -> p i c", i=4)
        e4b = [Buf(f"e4{i}") for i in range(4)]
        aug = AR.alloc(2048, F32)
        augb = Buf("aug")
        etmp = AR.alloc(2048, F32)
        etb = Buf("etmp")
        flat, wb = wload([(lambda f: v3(f, 16, 16), w_in[:, :, C_ALR:C_ALR + 16])])
        wv = v3(flat, 16, 16)
        P.emit("dve", lambda e: e.memset(aug[0:33, :], 0.0), writes=[augb])
        P.emit("dve", lambda e: e.memset(aug[32:33, :], 1.0), writes=[augb])
        bk, bb = nb()
        mm(bk[0:16, :], bb, [(wv[:, k, :], hT[:, k, :]) for k in range(16)], [wb, hTb])
        P.emit("act", lambda e, bk=bk: e.activation(out=aug[0:16, :], in_=bk[0:16, :], func=AF.Copy), reads=[bb], writes=[augb])
        zb = []
        for i in range(4):
            bk, bb = nb()
            mm(bk[:, :], bb, [(aug[0:33, i * 128:(i + 1) * 128], waug[0:33, :])], [augb, cb])
            zb.append((bk, bb))
        for i in range(4):
            bk, bb = zb[i]
            P.emit("act", lambda e, bk=bk, i=i: e.activation(out=g["sp"][:, i, :], in_=bk[:, :], func=AF.Exp, scale=-1.0), reads=[bb], writes=[g["spb"][i]])
        for i in range(4):
            P.emit("act", lambda e, i=i: e.activation(out=g["sp"][:, i, :], in_=g["sp"][:, i, :], func=AF.Ln, bias=1.0), reads=[g["spb"][i]], writes=[g["spb"][i]])
        zb = []
        for i in range(4):
            bk, bb = nb()
            mm(bk[:, :], bb, [(lmm[:, 1, :], g["sp"][:, i, :])], [cb, g["spb"][i]])
            zb.append((bk, bb))
        for i in range(4):
            bk, bb = zb[i]
            P.emit("act", lambda e, bk=bk, i=i: e.activation(out=e4[:, i, :], in_=bk[:, :], func=AF.Exp), reads=[bb], writes=[e4b[i]])
        flat, wb = wload([(lambda f: v3(f, 16, 512), w_in[:, :, C_K:C_K + 512])])
        wv = v3(flat, 16, 512)
        for i in range(4):
            bk, bb = nb()
            mm(bk[:, :], bb, [(hT[:, k, i * 128:(i + 1) * 128], wv[:, k, :]) for k in range(16)], [wb, hTb[i]])
            P.emit("dve", lambda e, bk=bk, i=i: e.tensor_tensor(out=g["kdec"][:, i, :], in0=bk[:, :], in1=e4[:, i, :], op=ALU.mult),
                   reads=[bb, e4b[i]], writes=[g["kdecb"][i]])
        for hv in range(2):
            flat, wb = wload([(lambda f: v3(f, 16, 512), w_in[:, :, C_V + hv * 512:C_V + (hv + 1) * 512])])
            wv = v3(flat, 16, 512)
            for i in range(4):
                bk, bb = nb()
                mm(bk[:, :], bb, [(hT[:, k, i * 128:(i + 1) * 128], wv[:, k, :]) for k in range(16)], [wb, hTb[i]])
                P.emit("act", lambda e, bk=bk, i=i, hv=hv: e.activation(out=g["v"][:, i, hv * 512:(hv + 1) * 512], in_=bk[:, :], func=AF.Copy),
                       reads=[bb], writes=[g["vb"][i]] if hv == 0 else (), partw=[g["vb"][i]] if hv == 1 else ())
        AR.reset(m0)
        return g

    def gla_bT(g, h, bTs, bTsb, with_dec=True):
        bk, bb = nb()
        for i in range(4):
            P.emit("pe", lambda e, bk=bk, i=i: e.matmul(bk[:, i * 128:(i + 1) * 128], lhsT=g["sp"][:, i, h * 128:(h + 1) * 128], rhs=lmm[:, 0, :], start=True, stop=True),
                   reads=[g["spb"][i], cb], writes=[bb] if i == 0 else (), partw=[bb] if i > 0 else (), inc=(i == 3))
        P.emit("act", lambda e, bk=bk: e.activation(out=bTs, in_=bk[:, :], func=AF.Copy), reads=[bb], writes=[bTsb])
        if with_dec:
            P.emit("act", lambda e: e.activation(out=dec[:, h, :], in_=bTs.rearrange("p (c t) -> p c t", c=8)[:, :, 63], func=AF.Exp),
                   reads=[bTsb], writes=[decb[h]])

    def gla_recur(g, h, cur, sbf=None, sbfb=None):
        for c in range(8):
            i, hf = c // 2, c % 2
            bk, bb = nb()
            mm(bk[:, 0:256], bb, [(g["kdec"][hf * 64:(hf + 1) * 64, i, h * 128:(h + 1) * 128], g["v"][hf * 64:(hf + 1) * 64, i, h * 256:(h + 1) * 256])],
               [g["kdecb"][i], g["vb"][i]])
            nx = 1 - cur
            P.emit("dve", lambda e, bk=bk, c=c, cur=cur, nx=nx: e.scalar_tensor_tensor(out=S[:, nx, h, :], in0=S[:, cur, h, :], scalar=dec[:, h, c:c + 1], in1=bk[:, 0:256], op0=ALU.mult, op1=ALU.add),
                   reads=[bb, Sb[cur][h], decb[h]], writes=[Sb[nx][h]])
            cur = nx
            if sbf is not None and c < 7:
                P.emit("act", lambda e, c=c, cur=cur: e.activation(out=sbf[:, c + 1, :], in_=S[:, cur, h, :], func=AF.Copy),
                       reads=[Sb[cur][h]], writes=[sbfb[c + 1]])
        return cur

    def gla_recur_all(g, sbfs=None, sbfbs=None):
        cur = 0
        for c in range(8):
            i, hf = c // 2, c % 2
            nx = 1 - cur
            bks = []
            for h in range(4):
                bk, bb = nb()
                mm(bk[:, 0:256], bb, [(g["kdec"][hf * 64:(hf + 1) * 64, i, h * 128:(h + 1) * 128], g["v"][hf * 64:(hf + 1) * 64, i, h * 256:(h + 1) * 256])],
                   [g["kdecb"][i], g["vb"][i]])
                bks.append((bk, bb))
            for h in range(4):
                bk, bb = bks[h]
                P.emit("dve", lambda e, bk=bk, c=c, cur=cur, nx=nx, h=h: e.scalar_tensor_tensor(out=S[:, nx, h, :], in0=S[:, cur, h, :], scalar=dec[:, h, c:c + 1], in1=bk[:, 0:256], op0=ALU.mult, op1=ALU.add),
                       reads=[bb, Sb[cur][h], decb[h]], writes=[Sb[nx][h]])
            if sbfs is not None and c < 7:
                for h in range(4):
                    P.emit("act", lambda e, c=c, nx=nx, h=h: e.activation(out=sbfs[h][:, c + 1, :], in_=S[:, nx, h, :], func=AF.Copy),
                           reads=[Sb[nx][h]], writes=[sbfbs[h][c + 1]])
            cur = nx
        return cur

    xloaded = [False]
    have_prefix = n_pre_pass > 0

    def prefix_pass(pi, last):
        P.barrier()
        AR.reset(BASE)
        if not xloaded[0]:
            load_x(xp_d, pi * TP)
        hT = AR.alloc(16384, BF16, "p (k t) -> p k t", k=16)
        hTb = [Buf(f"hT_{i}") for i in range(4)]
        stage_norm(0, hT, hTb)
        if last:
            load_x(x_d, 0)
        else:
            load_x(xp_d, (pi + 1) * TP)
        xloaded[0] = True
        P.barrier()
        g = gla_g0(hT, hTb, own=False)
        bTs = AR.alloc(2048, F32)
        bTsb = Buf("bTs")
        for h in range(4):
            gla_bT(g, h, bTs, bTsb)
        gla_recur_all(g)
        if last:
            P.emit("dve", lambda e: e.tensor_copy(out=hTh, in_=hT[:, :, 510:512]), reads=[hTb], writes=[hThb])

    def own_pass(pi, last):
        row0 = pi * TP
        dump = dump_all if pi == 0 else (lambda *a: None)
        P.barrier()
        if pi > 0:
            for en in ("act", "dve", "pe"):
                P.final_wait(en, ["stx0", "stx1"])
        AR.reset(BASE)
        if not xloaded[0]:
            load_x(x_d, row0)
        xloaded[0] = True
        hT = AR.alloc(16384, BF16, "p (k t) -> p k t", k=16)
        hTb = [Buf(f"hT_{i}") for i in range(4)]
        stage_norm(0, hT, hTb)
        dump("hT", hT, [hTb])
        if stop == "A":
            return

        P.barrier(engines=("sp",))
        if pi > 0:
            for en in ("act", "dve", "pe", "sp"):
                P.final_wait(en, ["stx2", "stx3"])
        ua = AR.alloc(8192, BF16, "p (k t) -> p k t", k=8)
        uab = Buf("ua")
        mB1 = AR.mark()
        pf = [AR.alloc(1024, F32) for _ in range(2)]
        pfb = [Buf("pf0"), Buf("pf1")]
        pbf = [AR.alloc(512, BF16) for _ in range(2)]
        pbfb = [Buf("pbf0"), Buf("pbf1")]
        ux = [AR.alloc(2048, F32) for _ in range(2)]
        uxb = [Buf("ux0"), Buf("ux1")]
        u = [AR.alloc(514 * 4, F32) for _ in range(2)]
        ub_ = [Buf("u0"), Buf("u1")]
        y = [AR.alloc(2048, F32) for _ in range(2)]
        yb = [Buf("y0"), Buf("y1")]
        halo_t = [AR.alloc(64, F32)[:, 0:2] for _ in range(2)]
        halo_tb = [Buf("halo_t0"), Buf("halo_t1")]
        for cc in range(8):
            q = cc % 2
            pieces = []
            for j, c0 in enumerate((C_AX, C_AB, C_AC)):
                pieces.append((lambda f, j=j: f[:, j * 2048:(j + 1) * 2048].rearrange("p (k c) -> p k c", k=16), w_in[:, :, c0 + cc * 128:c0 + (cc + 1) * 128]))
            flat, wb = wload(pieces)
            wx, wbb, wc = [flat[:, j * 2048:(j + 1) * 2048].rearrange("p (k c) -> p k c", k=16) for j in range(3)]
            bkx, bbx = nb()
            bkc, bbc = nb()
            bkb, bbb = nb()
            if cc == 0:
                for i_ in range(4):
                    ts_ = slice(i_ * 128, (i_ + 1) * 128)
                    for bk_, bb_, w_ in ((bkx, bbx, wx), (bkc, bbc, wc), (bkb, bbb, wbb)):
                        mm(bk_[:, ts_], bb_, [(w_[:, k, :], hT[:, k, ts_]) for k in range(16)], [wb, hTb[i_]], new=(i_ == 0))
            else:
                mm(bkx[:, :], bbx, [(wx[:, k, :], hT[:, k, :]) for k in range(16)], [wb, hTb])
                mm(bkc[:, :], bbc, [(wc[:, k, :], hT[:, k, :]) for k in range(16)], [wb, hTb])
                mm(bkb[:, :], bbb, [(wbb[:, k, :], hT[:, k, :]) for k in range(16)], [wb, hTb])
            P.emit("act", lambda e, bkx=bkx, q=q: e.activation(out=ux[q], in_=bkx[:, :], func=AF.Copy), reads=[bbx], writes=[uxb[q]])
            if pi == 0 and have_prefix:
                bkh, bbh = nb()
                mm(bkh[:, 0:2], bbh, [(wx[:, k, :], hTh[:, k, :]) for k in range(16)], [wb, hThb])
                mm(bkh[:, 2:4], bbh, [(wc[:, k, :], hTh[:, k, :]) for k in range(16)], [wb, hThb], new=False)
                P.emit("act", lambda e, bkh=bkh, q=q: e.activation(out=halo_t[q], in_=bkh[:, 0:2], func=AF.Copy), reads=[bbh], writes=[halo_tb[q]])
                P.emit("dve", lambda e, bkh=bkh, q=q: e.tensor_tensor(out=u[q][:, 0:2], in0=bkh[:, 2:4], in1=halo_t[q], op=ALU.mult),
                       reads=[bbh, halo_tb[q]], writes=[ub_[q]])
            else:
                P.emit("dve", lambda e, q=q, cc=cc: e.tensor_copy(out=u[q][:, 0:2], in_=uh[:, cc, :]), reads=[uhb], writes=[ub_[q]])
            P.emit("dve", lambda e, q=q, bkc=bkc: e.tensor_tensor(out=u[q][:, 2:514], in0=bkc[:, :], in1=ux[q], op=ALU.mult),
                   reads=[bbc, uxb[q]], partw=[ub_[q]])
            P.emit("dve", lambda e, q=q, cc=cc: e.tensor_scalar(out=y[q], in0=u[q][:, 0:512], scalar1=convw[:, cc, 0:1], scalar2=None, op0=ALU.mult),
                   reads=[ub_[q], cb], writes=[yb[q]])
            for j in (1, 2):
                P.emit("dve", lambda e, q=q, cc=cc, j=j: e.scalar_tensor_tensor(out=y[q], in0=u[q][:, j:j + 512], scalar=convw[:, cc, j:j + 1], in1=y[q], op0=ALU.mult, op1=ALU.add),
                       reads=[ub_[q], cb, yb[q]], writes=[yb[q]])
            P.emit("dve", lambda e, q=q, cc=cc, bkb=bkb: e.tensor_tensor(out=ua[:, cc, :], in0=bkb[:, :], in1=y[q], op=ALU.mult),
                   reads=[bbb, yb[q]], partw=[uab])
            P.emit("dve", lambda e, q=q, cc=cc: e.tensor_copy(out=uh[:, cc, :], in_=u[q][:, 512:514]), reads=[ub_[q]], writes=[uhb])
        pTb = Buf("pT")
        for i in range(4):
            q = i % 2
            P.dma("sp", f"ldp{q}", lambda e, i=i, q=q: e.dma_start(out=pf[q], in_=p_d[row0 + i * 128: row0 + (i + 1) * 128, :]), writes=[pfb[q]])
            P.emit("dve", lambda e, q=q: e.tensor_copy(out=pbf[q], in_=pf[q]), reads=[pfb[q]], writes=[pbfb[q]])
            bk, bb = nb()
            ptv = bk[:, :].bitcast(BF16)
            for j in range(2):
                P.emit("pe", lambda e, q=q, j=j, ptv=ptv: e.transpose(out=ptv[:, j * 128:(j + 1) * 128], in_=pbf[q][:, j * 128:(j + 1) * 128], identity=ident),
                       reads=[pbfb[q], cb] if j == 0 else (), writes=[bb] if j == 0 else (), inc=(j == 1))
            P.emit("act", lambda e, i=i, ptv=ptv: e.activation(out=pT[:, :, i * 128:(i + 1) * 128], in_=ptv[:, 0:256].rearrange("p (k t) -> p k t", k=2), func=AF.Copy),
                   reads=[bb], partw=[pTb])
        dump("ua", ua, [uab])
        if stop == "B1":
            return

        P.barrier()
        AR.reset(mB1)
        ubT = AR.alloc(8192, BF16, "p (k t) -> p k t", k=8)
        ubTb = Buf("ubT")
        mB2 = AR.mark()
        g = gla_g0(hT, hTb, own=True)
        if stop == "B2a":
            dump("sp", g["sp"], g["spb"])
            dump("kdec", g["kdec"], g["kdecb"])
            dump("v", g["v"], g["vb"])
            return
        ubt = AR.alloc(8192, BF16, "p (i c) -> p i c", i=4)
        ubtb = [Buf(f"ubt{i}") for i in range(4)]
        bTs = AR.alloc(2048, F32)
        bTsb = Buf("bTs")
        E = [AR.alloc(2048, F32) for _ in range(3)]
        Eb = [Buf(f"E{j}") for j in range(3)]
        QK = [[AR.alloc(1024, BF16) for _ in range(6)] for _ in range(2)]
        QKb = [[Buf(f"qk{s_}{j}") for j in range(6)] for s_ in range(2)]
        SBF = [AR.alloc(8 * 256 * 2, BF16, "p (c v) -> p c v", c=8) for _ in range(4)]
        SBFb = [[Buf(f"sbf{s_}{c}") for c in range(8)] for s_ in range(4)]
        SO = [AR.alloc(4 * 256 * 4, F32, "p (i v) -> p i v", i=4) for _ in range(2)]
        SOb = [[Buf(f"so{s_}{i}") for i in range(4)] for s_ in range(2)]
        atmp = [AR.alloc(512, BF16, "p (a b) -> p a b", a=2) for _ in range(2)]
        atb = [Buf("atmp0"), Buf("atmp1")]
        attT = [AR.alloc(256, BF16) for _ in range(2)]
        attb = [Buf("attT0"), Buf("attT1")]
        on = AR.alloc(1024, F32)
        onb = Buf("on")
        sg = AR.alloc(1024, F32)
        sgb = Buf("sg")
        junk2 = AR.alloc(512, BF16)
        jb2 = Buf("junk2")
        for s_ in range(2):
            P.emit("dve", lambda e, s_=s_: e.memset(QK[s_][4], 0.0), writes=[QKb[s_][4]])
            P.emit("dve", lambda e, s_=s_: e.memset(QK[s_][5], 0.0), writes=[QKb[s_][5]])

        def gla_part1(h):
            s_ = h % 2
            qf, qr, kf, kr, qbA, qbB = QK[s_]
            qkb = QKb[s_]
            gla_bT(g, h, bTs, bTsb, with_dec=False)
            bv = bTs.rearrange("p (c t) -> p c t", c=8)
            P.emit("dve", lambda e, bv=bv: e.tensor_tensor(out=E[0].rearrange("p (c t) -> p c t", c=8), in0=bv, in1=bv[:, :, 32:33].to_broadcast([128, 8, 64]), op=ALU.subtract),
                   reads=[bTsb], writes=[Eb[0]])
            P.emit("act", lambda e: e.activation(out=E[1], in_=E[0], func=AF.Exp, scale=-1.0), reads=[Eb[0]], writes=[Eb[1]])
            P.emit("act", lambda e: e.activation(out=E[0], in_=E[0], func=AF.Exp), reads=[Eb[0]], writes=[Eb[0]])
            P.emit("act", lambda e: e.activation(out=E[2], in_=bTs, func=AF.Exp), reads=[bTsb], writes=[Eb[2]])
            flat, wb = wload([(lambda f: f[:, 0:2048].rearrange("p (k c) -> p k c", k=16), w_in[:, :, C_Q + h * 128:C_Q + (h + 1) * 128]),
                              (lambda f: f[:, 2048:4096].rearrange("p (k c) -> p k c", k=16), w_in[:, :, C_K + h * 128:C_K + (h + 1) * 128])])
            wq = flat[:, 0:2048].rearrange("p (k c) -> p k c", k=16)
            wk = flat[:, 2048:4096].rearrange("p (k c) -> p k c", k=16)
            bkq, bbq = nb()
            mm(bkq[:, :], bbq, [(wq[:, k, :], hT[:, k, :]) for k in range(16)], [wb, hTb])
            bkk, bbk = nb()
            mm(bkk[:, :], bbk, [(wk[:, k, :], hT[:, k, :]) for k in range(16)], [wb, hTb])
            P.emit("dve", lambda e, bkq=bkq: e.scalar_tensor_tensor(out=qf, in0=bkq[:, :], scalar=QSCALE, in1=E[0], op0=ALU.mult, op1=ALU.mult),
                   reads=[bbq, Eb[0]], writes=[qkb[0]])
            P.emit("dve", lambda e, bkq=bkq: e.scalar_tensor_tensor(out=qr, in0=bkq[:, :], scalar=QSCALE, in1=E[1], op0=ALU.mult, op1=ALU.mult),
                   reads=[bbq, Eb[1]], writes=[qkb[1]])
            for par, qb_, qi in ((0, qbA, 4), (1, qbB, 5)):
                P.emit("dve", lambda e, bkq=bkq, par=par, qb_=qb_: e.scalar_tensor_tensor(
                    out=qb_.rearrange("p (i a t) -> p i a t", i=4, a=2)[:, :, par, :],
                    in0=bkq[:, :].rearrange("p (i a t) -> p i a t", i=4, a=2)[:, :, par, :], scalar=QSCALE,
                    in1=E[2].rearrange("p (i a t) -> p i a t", i=4, a=2)[:, :, par, :], op0=ALU.mult, op1=ALU.mult),
                    reads=[bbq, Eb[2]], writes=[qkb[qi]])
            P.emit("dve", lambda e, bkk=bkk: e.tensor_tensor(out=kf, in0=bkk[:, :], in1=E[1], op=ALU.mult), reads=[bbk, Eb[1]], writes=[qkb[2]])
            P.emit("dve", lambda e, bkk=bkk: e.tensor_tensor(out=kr, in0=bkk[:, :], in1=E[0], op=ALU.mult), reads=[bbk, Eb[0]], writes=[qkb[3]])

        def gla_part2(h):
            s_ = h % 2
            qf, qr, kf, kr, qbA, qbB = QK[s_]
            qkb = QKb[s_]
            sbf, sbfb = SBF[h], SBFb[h]
            flat_o, wob = wload([(lambda f: v3(f, 16, 256), w_in[:, :, C_OG + h * 256:C_OG + (h + 1) * 256])])
            wog = v3(flat_o, 16, 256)

            def og_stage(i):
                ts = slice(i * 128, (i + 1) * 128)
                bkg, bbg = nb()
                mm(bkg[:, 0:256], bbg, [(hT[:, k, ts], wog[:, k, :]) for k in range(16)], [wob, hTb[i]])
                P.emit("act", lambda e, bkg=bkg: e.activation(out=sg, in_=bkg[:, 0:256], func=AF.Exp, scale=-1.0), reads=[bbg], writes=[sgb])
                P.emit("act", lambda e: e.activation(out=sg, in_=sg, func=AF.Ln, bias=1.0), reads=[sgb], writes=[sgb])
                P.emit("act", lambda e: e.activation(out=sg, in_=sg, func=AF.Exp, scale=-1.0), reads=[sgb], writes=[sgb])
                P.emit("dve", lambda e, bkg=bkg, i=i, s_=s_: e.tensor_tensor(out=SO[s_][:, i, :], in0=bkg[:, 0:256], in1=sg, op=ALU.mult),
                       reads=[bbg, sgb], writes=[SOb[s_][i]])

            def att_stage(i):
                ts = slice(i * 128, (i + 1) * 128)
                a_ = i % 2
                bk, bb = nb()
                mm(bk[:, 0:128], bb, [(kf[:, ts], qf[:, ts])], [qkb[2], qkb[0]])
                mm(bk[:, 128:256], bb, [(kr[:, ts], qr[:, ts])], [qkb[3], qkb[1]], new=False)
                P.emit("dve", lambda e, bk=bk, a_=a_: e.tensor_tensor(out=atmp[a_], in0=bk[:, 0:256].rearrange("p (a b) -> p a b", a=2), in1=mfr, op=ALU.mult),
                       reads=[bb, cb], writes=[atb[a_]])
                P.emit("dve", lambda e, a_=a_: e.tensor_tensor(out=attT[a_], in0=atmp[a_][:, 0, :], in1=atmp[a_][:, 1, :], op=ALU.add), reads=[atb[a_]], writes=[attb[a_]])

            def out_stage(i):
                ts = slice(i * 128, (i + 1) * 128)
                a_ = i % 2
                bko, bbo = nb()
                mm(bko[:, 0:256], bbo, [(attT[a_], g["v"][:, i, h * 256:(h + 1) * 256]), (qbA[:, ts], sbf[:, 2 * i, :]), (qbB[:, ts], sbf[:, 2 * i + 1, :])],
                   [attb[a_], g["vb"][i], qkb[4], qkb[5], sbfb[2 * i], sbfb[2 * i + 1]])
                hs = hstat[:, i, h * 4:(h + 1) * 4]
                P.emit("act", lambda e, bko=bko, hs=hs: e.activation(out=junk2, in_=bko[:, 0:256], func=AF.Square, accum_out=hs[:, 0:1]),
                       reads=[bbo], writes=[jb2, hstb[i]])
                rstd_ops(hs, hstb[i], 256)
                P.emit("dve", lambda e, bko=bko, hs=hs: e.scalar_tensor_tensor(out=on, in0=bko[:, 0:256], scalar=hs[:, 2:3], in1=hgb, op0=ALU.mult, op1=ALU.mult),
                       reads=[bbo, hstb[i], cb], writes=[onb])
                P.emit("dve", lambda e, i=i, s_=s_: e.tensor_tensor(out=ubt[:, i, h * 256:(h + 1) * 256], in0=on, in1=SO[s_][:, i, :], op=ALU.mult),
                       reads=[onb, SOb[s_][i]], partw=[ubtb[i]])

            att_stage(0)
            att_stage(1)
            og_stage(0)
            out_stage(0)
            att_stage(2)
            og_stage(1)
            out_stage(1)
            att_stage(3)
            og_stage(2)
            out_stage(2)
            og_stage(3)
            out_stage(3)

        for h_ in range(4):
            gla_bT(g, h_, bTs, bTsb)
            P.emit("act", lambda e, h_=h_: e.activation(out=SBF[h_][:, 0, :], in_=S[:, 0, h_, :], func=AF.Copy), reads=[Sb[0][h_]], writes=[SBFb[h_][0]])
        gla_recur_all(g, SBF, SBFb)
        gla_part1(0)
        gla_part1(1)
        gla_part2(0)
        gla_part1(2)
        gla_part2(1)
        gla_part1(3)
        gla_part2(2)
        gla_part2(3)
        for i in range(4):
            bk, bb = nb()
            ptv = bk[:, :].bitcast(BF16)
            for j in range(8):
                P.emit("pe", lambda e, i=i, j=j, ptv=ptv: e.transpose(out=ptv[:, j * 128:(j + 1) * 128], in_=ubt[:, i, j * 128:(j + 1) * 128], identity=ident),
                       reads=[ubtb[i], cb] if j == 0 else (), writes=[bb] if j == 0 else (), inc=(j == 7))
            P.emit("act", lambda e, i=i, ptv=ptv: e.activation(out=ubT[:, :, i * 128:(i + 1) * 128], in_=ptv.rearrange("p (k t) -> p k t", k=8), func=AF.Copy),
                   reads=[bb], partw=[ubTb])
        dump("ubT", ubT, [ubTb])
        if stop == "B2":
            return

        P.barrier()
        AR.reset(mB2)
        mixT = AR.alloc(16384, BF16, "p (k t) -> p k t", k=16)
        mixb = Buf("mixT")
        mB3 = AR.mark()
        sga = [AR.alloc(2048, F32) for _ in range(2)]
        sgab = [Buf("sga0"), Buf("sga1")]
        sgb2 = [AR.alloc(2048, F32) for _ in range(2)]
        sgbb = [Buf("sgb0"), Buf("sgb1")]
        for dc in range(16):
            q = dc % 2
            cs = slice(dc * 128, (dc + 1) * 128)
            pieces = [(lambda f: f[:, 0:1024].rearrange("p (k c) -> p k c", k=8), w_a_out[:, :, cs]),
                      (lambda f: f[:, 1024:3072].rearrange("p (k c) -> p k c", k=16), w_in[:, :, C_GA + dc * 128:C_GA + (dc + 1) * 128]),
                      (lambda f: f[:, 3072:4096].rearrange("p (k c) -> p k c", k=8), w_b_out[:, :, cs]),
                      (lambda f: f[:, 4096:6144].rearrange("p (k c) -> p k c", k=16), w_in[:, :, C_GB + dc * 128:C_GB + (dc + 1) * 128])]
            flat, wb = wload(pieces)
            wa = flat[:, 0:1024].rearrange("p (k c) -> p k c", k=8)
            wga = flat[:, 1024:3072].rearrange("p (k c) -> p k c", k=16)
            wbo = flat[:, 3072:4096].rearrange("p (k c) -> p k c", k=8)
            wgb = flat[:, 4096:6144].rearrange("p (k c) -> p k c", k=16)
            bya, bbya = nb()
            mm(bya[:, :], bbya, [(wa[:, k, :], ua[:, k, :]) for k in range(8)], [wb, uab])
            bga, bbga = nb()
            mm(bga[:, :], bbga, [(wga[:, k, :], hT[:, k, :]) for k in range(16)], [wb, hTb])
            byb, bbyb = nb()
            mm(byb[:, :], bbyb, [(wbo[:, k, :], ubT[:, k, :]) for k in range(8)], [wb, ubTb])
            bgb, bbgb = nb()
            mm(bgb[:, :], bbgb, [(wgb[:, k, :], hT[:, k, :]) for k in range(16)], [wb, hTb])
            P.emit("act", lambda e, bga=bga, q=q: e.activation(out=sga[q], in_=bga[:, :], func=AF.Sigmoid), reads=[bbga], writes=[sgab[q]])
            P.emit("act", lambda e, bgb=bgb, q=q: e.activation(out=sgb2[q], in_=bgb[:, :], func=AF.Sigmoid), reads=[bbgb], writes=[sgbb[q]])
            P.emit("dve", lambda e, bya=bya, q=q: e.tensor_tensor(out=sga[q], in0=bya[:, :], in1=sga[q], op=ALU.mult), reads=[bbya, sgab[q]], writes=[sgab[q]])
            P.emit("dve", lambda e, byb=byb, q=q: e.tensor_tensor(out=sgb2[q], in0=byb[:, :], in1=sgb2[q], op=ALU.mult), reads=[bbyb, sgbb[q]], writes=[sgbb[q]])
            P.emit("dve", lambda e, q=q, dc=dc: e.tensor_tensor(out=mixT[:, dc, :], in0=sga[q], in1=sgb2[q], op=ALU.add), reads=[sgab[q], sgbb[q]], partw=[mixb])
        dump("mixT", mixT, [mixb])
        if stop == "B3":
            return

        P.barrier()
        AR.reset(mB3)
        m_sb = AR.alloc(4 * D * 4, F32, "p (i c) -> p i c", i=4)
        msb_bufs = [Buf(f"msb{i}") for i in range(4)]
        junk = AR.alloc(1024, BF16)
        jb = Buf("junk")
        gb, gbb = load_gb(0)

        def proj_tok(act, actb, nk_list, wsrc, m_sb, msb_bufs, junk, jb, gb, gbb):
            for cbk in range(4):
                cs = slice(cbk * 512, (cbk + 1) * 512)
                bks = [nb() for _ in range(4)]
                k0 = 0
                for part, nk in enumerate(nk_list):
                    flat, wb = wload([(lambda f, nk=nk: v3(f, nk, 512), wsrc[:, k0:k0 + nk, cs])])
                    wv = v3(flat, nk, 512)
                    for i in range(4):
                        bk, bb = bks[i]
                        for kk in range(nk):
                            first = (part == 0 and kk == 0)
                            lastk = (part == len(nk_list) - 1 and kk == nk - 1)
                            P.emit("pe", lambda e, bk=bk, i=i, kk=kk, k0=k0, first=first, lastk=lastk, wv=wv: e.matmul(
                                bk[:, :], lhsT=act[:, k0 + kk, i * 128:(i + 1) * 128], rhs=wv[:, kk, :], start=first, stop=lastk),
                                reads=[wb, actb] if kk == 0 else (), writes=[bb] if first else (), partw=[bb] if (kk == 0 and not first) else (),
                                inc=(kk == nk - 1))
                    k0 += nk
                for i in range(4):
                    bk, bb = bks[i]
                    P.emit("act", lambda e, bk=bk, i=i, cbk=cbk: e.activation(out=junk, in_=bk[:, :], func=AF.Square, accum_out=ssqp[:, i * 8 + cbk:i * 8 + cbk + 1]),
                           reads=[bb], writes=[jb], partw=[ssqpb])
                    P.emit("dve", lambda e, bk=bk, i=i, cs=cs: e.tensor_tensor(out=m_sb[:, i, cs], in0=bk[:, :], in1=gb[:, cs], op=ALU.mult),
                           reads=[bb, gbb], partw=[msb_bufs[i]])
                    if cbk == 3:
                        post_norm_tile(i, m_sb, msb_bufs, 4)

        proj_tok(mixT, mixb, [16], w_mix, m_sb, msb_bufs, junk, jb, gb, gbb)
        dump("x1", X, xb)
        if stop == "B4":
            return

        P.barrier()
        AR.reset(BASE)
        hid = AR.alloc(44 * 512 * 2, BF16, "p (k t) -> p k t", k=44)
        hidb = Buf("hid")
        mC = AR.mark()
        hT2 = AR.alloc(16384, BF16, "p (k t) -> p k t", k=16)
        hTb2 = [Buf(f"hT2_{i}") for i in range(4)]
        stage_norm(1, hT2, hTb2)
        sg2 = [AR.alloc(2048, F32) for _ in range(2)]
        sg2b = [Buf("sg0"), Buf("sg1")]
        for fb in range(22):
            cs = slice(fb * 256, (fb + 1) * 256)
            flat_w, wbw = wload([(lambda f: f[:, 0:4096].rearrange("p (k c) -> p k c", k=16), w_fg[:, :, cs]),
                                 (lambda f: f[:, 4096:8192].rearrange("p (k c) -> p k c", k=16), w_fu[:, :, cs])])
            wg_ = flat_w[:, 0:4096].rearrange("p (k c) -> p k c", k=16)
            wu_ = flat_w[:, 4096:8192].rearrange("p (k c) -> p k c", k=16)
            gus = [(nb(), nb()) for _ in range(2)]
            if fb == 0:
                for i_ in range(4):
                    ts_ = slice(i_ * 128, (i_ + 1) * 128)
                    for j in range(2):
                        (bg, bbg), (bu, bbu) = gus[j]
                        mm(bg[:, ts_], bbg, [(wg_[:, k, j * 128:(j + 1) * 128], hT2[:, k, ts_]) for k in range(16)], [wbw, hTb2[i_]], new=(i_ == 0))
                        mm(bu[:, ts_], bbu, [(wu_[:, k, j * 128:(j + 1) * 128], hT2[:, k, ts_]) for k in range(16)], [wbw, hTb2[i_]], new=(i_ == 0))
            for j in range(2):
                q = j % 2
                fc = fb * 2 + j
                (bg, bbg), (bu, bbu) = gus[j]
                if fb > 0:
                    mm(bg[:, :], bbg, [(wg_[:, k, j * 128:(j + 1) * 128], hT2[:, k, :]) for k in range(16)], [wbw, hTb2])
                    mm(bu[:, :], bbu, [(wu_[:, k, j * 128:(j + 1) * 128], hT2[:, k, :]) for k in range(16)], [wbw, hTb2])
                P.emit("act", lambda e, bg=bg, q=q: e.activation(out=sg2[q], in_=bg[:, :], func=AF.Silu), reads=[bbg], writes=[sg2b[q]])
                P.emit("dve", lambda e, bu=bu, q=q, fc=fc: e.tensor_tensor(out=hid[:, fc, :], in0=bu[:, :], in1=sg2[q], op=ALU.mult),
                       reads=[bbu, sg2b[q]], partw=[hidb])
        P.barrier()
        AR.reset(mC)
        m_sb2 = AR.alloc(4 * D * 4, F32, "p (i c) -> p i c", i=4)
        msb_bufs2 = [Buf(f"msb{i}") for i in range(4)]
        junk3 = AR.alloc(1024, BF16)
        jb3 = Buf("junk")
        gb2, gbb2 = load_gb(1)
        proj_tok(hid, hidb, [15, 15, 14], w_fd, m_sb2, msb_bufs2, junk3, jb3, gb2, gbb2)
        dump("x2", X, xb)
        if stop == "C":
            return

        P.barrier()
        AR.reset(BASE)
        hT3 = AR.alloc(16384, BF16, "p (k t) -> p k t", k=16)
        hTb3 = [Buf(f"hT3_{i}") for i in range(4)]
        m_sb3 = AR.alloc(4 * D * 4, F32, "p (i c) -> p i c", i=4)
        msb_bufs3 = [Buf(f"msb{i}") for i in range(4)]
        junk4 = AR.alloc(1024, BF16)
        jb4 = Buf("junk")
        gb3, gbb3 = load_gb(2)
        sgd = [AR.alloc(2048, F32) for _ in range(2)]
        sgdb = [Buf("sgd0"), Buf("sgd1")]
        mD = AR.mark()
        stage_norm(2, hT3, hTb3)
        nxt = (x_d, (pi + 1) * TP) if not last else None
        for cbk in range(8):
            cs = slice(cbk * 256, (cbk + 1) * 256)
            flat_w, wbw = wload([(lambda f: f[:, 0:4096].rearrange("p (k c) -> p k c", k=16), w_pg[:, :, cs]),
                                 (lambda f: f[:, 4096:4608].rearrange("p (k c) -> p k c", k=2), w_pp[:, :, cs])])
            wg_ = flat_w[:, 0:4096].rearrange("p (k c) -> p k c", k=16)
            wp_ = flat_w[:, 4096:4608].rearrange("p (k c) -> p k c", k=2)
            for i in range(4):
                q = i % 2
                ts = slice(i * 128, (i + 1) * 128)
                bg, bbg = nb()
                mm(bg[:, 0:256], bbg, [(hT3[:, k, ts], wg_[:, k, :]) for k in range(16)], [wbw, hTb3[i]])
                mm(bg[:, 256:512], bbg, [(pT[:, k, ts], wp_[:, k, :]) for k in range(2)], [wbw, pTb], new=False)
                P.emit("act", lambda e, bg=bg, q=q: e.activation(out=sgd[q][:, 0:256], in_=bg[:, 0:256], func=AF.Sigmoid), reads=[bbg], writes=[sgdb[q]])
                P.emit("dve", lambda e, bg=bg, q=q: e.tensor_tensor(out=sgd[q][:, 0:256], in0=bg[:, 256:512], in1=sgd[q][:, 0:256], op=ALU.mult),
                       reads=[bbg, sgdb[q]], writes=[sgdb[q]])
                P.emit("act", lambda e, q=q, i=i, cbk=cbk: e.activation(out=junk4[:, 0:256], in_=sgd[q][:, 0:256], func=AF.Square, accum_out=ssqp[:, i * 8 + cbk:i * 8 + cbk + 1]),
                       reads=[sgdb[q]], writes=[jb4], partw=[ssqpb])
                P.emit("dve", lambda e, q=q, i=i, cs=cs: e.tensor_tensor(out=m_sb3[:, i, cs], in0=sgd[q][:, 0:256], in1=gb3[:, cs], op=ALU.mult),
                       reads=[sgdb[q], gbb3], partw=[msb_bufs3[i]])
                if cbk == 7:
                    post_norm_tile(i, m_sb3, msb_bufs3, 8, last_store_row0=row0, next_load=nxt)

    for pi in range(n_pre_pass):
        prefix_pass(pi + (2 - n_pre_pass), last=(pi == n_pre_pass - 1))
    for pi in range(n_own_pass):
        own_pass(pi, last=(pi == n_own_pass - 1))
    sems = [f"stx{i}" for i in range(4)] + (["dbg"] if dbg_out else [])
    P.final_wait("sp", sems)
    P.replay()
    return nc


def make_consts():
    s = np.arange(128)[:, None]
    t = np.arange(128)[None, :]
    same = (s // 64) == (t // 64)
    mf = (same & (s <= t)).astype(np.float32)
    mr = (same & (s > t)).astype(np.float32)
    mfr = np.concatenate([mf, mr], axis=1).astype(ml_dtypes.bfloat16)
    lm = np.where(same & (s <= t), -1.0 / 16.0, 0.0).astype(np.float32)
    lrev = np.where(same & (s > t), -1.0 / 16.0, 0.0).astype(np.float32)
    lmm = np.concatenate([lm, lrev], axis=1).astype(np.float32)
    ident = np.eye(128, dtype=np.float32).astype(ml_dtypes.bfloat16)
    return ident, mfr, lmm


def make_in_maps(inputs):
    f = lambda a: np.ascontiguousarray(np.asarray(a), dtype=np.float32)
    x = f(inputs["x"])
    p = f(inputs["p"])[0]
    col = lambda g: f(g)[0].reshape(16, 128).T
    gcol = np.ascontiguousarray(np.concatenate([col(inputs["g_pre_mix"]), col(inputs["g_pre_ffn"]), col(inputs["g_pre_ple"])], axis=1))
    gpost = np.ascontiguousarray(np.stack([f(inputs["g_post_mix"])[0], f(inputs["g_post_ffn"])[0], f(inputs["g_post_ple"])[0]], axis=0))
    cw = f(inputs["conv_w"])[0]
    convw = np.ascontiguousarray(cw.reshape(3, 8, 128).transpose(2, 1, 0).reshape(128, 24))
    waug = np.zeros((33, 512), np.float32)
    waug[0:16] = f(inputs["w_alpha_up"])[0]
    waug[32] = f(inputs["b_alpha_up"])[0]
    hg = f(inputs["gla_head_gain"])[0].reshape(1, 256)
    ident, mfr, lmm = make_consts()
    shared = {
        "w_in": f(inputs["w_in"])[0], "w_a_out": f(inputs["w_a_out"])[0], "w_b_out": f(inputs["w_b_out"])[0],
        "w_mix_out": f(inputs["w_mix_out"])[0], "w_ff_gate": f(inputs["w_ff_gate"])[0], "w_ff_up": f(inputs["w_ff_up"])[0],
        "w_ff_down": f(inputs["w_ff_down"])[0], "w_ple_gate": f(inputs["w_ple_gate"])[0], "w_ple_proj": f(inputs["w_ple_proj"])[0],
        "gcol": gcol, "gpost": gpost, "convw": convw, "waug": waug, "hg": hg, "ident": ident, "mfr": mfr, "lm": lmm,
    }
    in_maps = []
    for c in range(8):
        b, half = c // 2, c % 2
        m = dict(shared)
        m["x"] = np.ascontiguousarray(x[b, half * NTOK:(half + 1) * NTOK])
        m["xp"] = np.ascontiguousarray(x[b, 0:NTOK]) if half == 1 else np.zeros((NTOK, D), np.float32)
        m["p"] = np.ascontiguousarray(p[b, half * NTOK:(half + 1) * NTOK])
        in_maps.append(m)
    return in_maps


def kernel(**inputs):
    in_maps = make_in_maps(inputs)
    nc = build_program()
    res = run_bass_kernel_spmd(nc, in_maps, core_ids=list(range(8)))
    out = np.empty((4, 2048, D), np.float32)
    for c in range(8):
        b, half = c // 2, c % 2
        out[b, half * NTOK:(half + 1) * NTOK] = np.asarray(res.results[c]["out"], dtype=np.float32)
    return out
```

```python
import numpy as np
from contextlib import ExitStack
import ml_dtypes
import concourse.bass as bass
import concourse.mybir as mybir
from concourse.bass_utils import run_bass_kernel_spmd

F32 = mybir.dt.float32
BF16 = mybir.dt.bfloat16
AF = mybir.ActivationFunctionType
ALU = mybir.AluOpType

D = 2048
DFF = 5632
NTOK = 1024
TP = 512
EPS = 1e-6
C_AX, C_AB, C_AC, C_Q, C_K, C_V, C_OG, C_ALR, C_GA, C_GB = 0, 1024, 2048, 3072, 3584, 4096, 5120, 6144, 6160, 8208
INC = 10256
QSCALE = 128 ** -0.5


class Buf:
    __slots__ = ("name", "w", "r")

    def __init__(self, name):
        self.name = name
        self.w = {}
        self.r = {}


class Eng:
    def __init__(self, name, sem):
        self.name = name
        self.sem = sem
        self.count = 0
        self.waited = {}
        self.prog = []


class Prog:
    COMPUTE = ("pe", "act", "dve")

    def __init__(self, nc):
        self.nc = nc
        self.semcount = {}
        self.engs = {
            "pe": Eng("pe", "c_pe"),
            "act": Eng("act", "c_act"),
            "dve": Eng("dve", "c_dve"),
            "sp": Eng("sp", None),
            "pool": Eng("pool", None),
        }
        self.semnames = ["c_pe", "c_act", "c_dve"]

    @staticmethod
    def _flat(bs):
        out = []
        for b in bs:
            if isinstance(b, (list, tuple)):
                out.extend(Prog._flat(b))
            else:
                out.append(b)
        return out

    def _needs(self, E, reads, writes, partw):
        need = {}
        for b in reads:
            for s, v in b.w.items():
                if need.get(s, 0) < v:
                    need[s] = v
        for b in writes:
            for s, v in b.w.items():
                if need.get(s, 0) < v:
                    need[s] = v
            for s, v in b.r.items():
                if need.get(s, 0) < v:
                    need[s] = v
        for b in partw:
            for s, v in b.r.items():
                if need.get(s, 0) < v:
                    need[s] = v
        waits = []
        for s, v in need.items():
            if E.name == "pe" and s == "c_pe":
                continue
            if E.waited.get(s, 0) < v:
                waits.append((s, v))
                E.waited[s] = v
        return waits

    def _mark(self, ev, reads, writes, partw):
        for b in writes:
            b.w = {ev[0]: ev[1]}
            b.r = {}
        for b in partw:
            if b.w.get(ev[0], 0) < ev[1]:
                b.w[ev[0]] = ev[1]
        for b in reads:
            if b.r.get(ev[0], 0) < ev[1]:
                b.r[ev[0]] = ev[1]

    def emit(self, en, fn, reads=(), writes=(), partw=(), inc=True):
        E = self.engs[en]
        reads, writes, partw = self._flat(reads), self._flat(writes), self._flat(partw)
        if en != "pe":
            bank_r = [b for b in reads if b.name.startswith("bank")]
            if bank_r:
                reads = [b for b in reads if not b.name.startswith("bank")]
                writes = list(writes) + bank_r
        waits = self._needs(E, reads, writes, partw)
        if inc:
            E.count += 1
            ev = (E.sem, E.count)
            incs = [(E.sem, 1)]
        else:
            assert en == "pe"
            ev = (E.sem, E.count + 1)
            incs = []
        E.prog.append((waits, fn, incs))
        self._mark(ev, reads, writes, partw)

    def dma(self, en, sem, fn, reads=(), writes=(), partw=()):
        E = self.engs[en]
        if sem not in self.semcount:
            self.semcount[sem] = 0
            self.semnames.append(sem)
        reads, writes, partw = self._flat(reads), self._flat(writes), self._flat(partw)
        waits = self._needs(E, reads, writes, partw)
        self.semcount[sem] += 16
        ev = (sem, self.semcount[sem])
        E.prog.append((waits, fn, [(sem, 16)]))
        self._mark(ev, reads, writes, partw)

    def barrier(self, engines=("pe", "act", "dve", "sp")):
        for en in engines:
            E = self.engs[en]
            waits = []
            for fn_ in self.COMPUTE:
                F = self.engs[fn_]
                if F is E and en == "pe":
                    continue
                if E.waited.get(F.sem, 0) < F.count:
                    waits.append((F.sem, F.count))
                    E.waited[F.sem] = F.count
            if waits:
                E.prog.append((waits, None, []))

    def final_wait(self, en, sems):
        E = self.engs[en]
        waits = [(s, self.semcount[s]) for s in sems if self.semcount.get(s, 0) > 0]
        E.prog.append((waits, None, []))

    def replay(self):
        nc = self.nc
        with ExitStack() as st:
            H = {}
            for s in self.semnames:
                H[s] = st.enter_context(nc.semaphore(s))
            block = st.enter_context(nc.Block())

            def run(E):
                def body(e):
                    for waits, fn, incs in E.prog:
                        for s, v in waits:
                            e.wait_ge(H[s], v)
                        if fn is not None:
                            ins = fn(e)
                            for s, a in incs:
                                ins.then_inc(H[s], a)
                return body

            block.tensor(run(self.engs["pe"]))
            block.scalar(run(self.engs["act"]))
            block.vector(run(self.engs["dve"]))
            block.sync(run(self.engs["sp"]))
            block.gpsimd(run(self.engs["pool"]))


class Arena:
    def __init__(self, nc, nbytes):
        self.t = nc.alloc_sbuf_tensor("arena", [128, nbytes // 4], F32)
        self.nbytes = nbytes
        self.top = 0
        self.peak = 0

    def alloc(self, nbytes, dt=F32, pat=None, **kw):
        off = self.top
        self.top += (nbytes + 63) // 64 * 64
        assert self.top <= self.nbytes, f"arena overflow {self.top} > {self.nbytes}"
        self.peak = max(self.peak, self.top)
        v = self.t[:, off // 4:(off + nbytes) // 4]
        if dt != F32:
            v = v.bitcast(dt)
        if pat is not None:
            v = v.rearrange(pat, **kw)
        return v

    def mark(self):
        return self.top

    def reset(self, m):
        self.top = m


def build_program(n_pre_pass=2, n_own_pass=2, dbg=None, stop=None):
    nc = bass.Bass("TRN2", target_bir_lowering=False)
    P = Prog(nc)

    def din(name, shape, dt=F32):
        return nc.dram_tensor(name, list(shape), dt, kind="ExternalInput").ap()

    x_d = din("x", [NTOK, D])
    xp_d = din("xp", [NTOK, D])
    p_d = din("p", [NTOK, 256])
    w_in = din("w_in", [D, INC]).rearrange("(k p) c -> p k c", p=128)
    w_a_out = din("w_a_out", [1024, D]).rearrange("(k p) c -> p k c", p=128)
    w_b_out = din("w_b_out", [1024, D]).rearrange("(k p) c -> p k c", p=128)
    w_mix = din("w_mix_out", [D, D]).rearrange("(k p) c -> p k c", p=128)
    w_fg = din("w_ff_gate", [D, DFF]).rearrange("(k p) c -> p k c", p=128)
    w_fu = din("w_ff_up", [D, DFF]).rearrange("(k p) c -> p k c", p=128)
    w_fd = din("w_ff_down", [DFF, D]).rearrange("(k p) c -> p k c", p=128)
    w_pg = din("w_ple_gate", [D, D]).rearrange("(k p) c -> p k c", p=128)
    w_pp = din("w_ple_proj", [256, D]).rearrange("(k p) c -> p k c", p=128)
    gcol_d = din("gcol", [128, 48])
    gpost_d = din("gpost", [3, D])
    convw_d = din("convw", [128, 24])
    waug_d = din("waug", [33, 512])
    hg_d = din("hg", [1, 256])
    ident_d = din("ident", [128, 128], BF16)
    mfr_d = din("mfr", [128, 256], BF16)
    lm_d = din("lm", [128, 256])
    out_d = nc.dram_tensor("out", [NTOK, D], F32, kind="ExternalOutput").ap()
    dbg_out = {}
    if dbg:
        for name, (shape, dt) in dbg.items():
            dbg_out[name] = nc.dram_tensor("dbg_" + name, list(shape), dt, kind="ExternalOutput").ap()

    AR = Arena(nc, 206 * 1024)
    banks = [nc.alloc_psum_tensor(f"bank{i}", [128, 512], F32) for i in range(8)]
    bankbufs = [Buf(f"bank{i}") for i in range(8)]
    bank_i = [0]

    def nb():
        i = bank_i[0] % 8
        bank_i[0] += 1
        return banks[i], bankbufs[i]

    def dump_all(name, ap, bufs):
        if name in dbg_out:
            P.dma("sp", "dbg", lambda e, o=dbg_out[name], a=ap: e.dma_start(out=o, in_=a), reads=bufs)

    ident = AR.alloc(256, BF16)
    mfr = AR.alloc(512, BF16, "p (a b) -> p a b", a=2)
    lmm = AR.alloc(1024, F32, "p (a b) -> p a b", a=2)
    gcol = AR.alloc(192, F32, "p (a b) -> p a b", a=3)
    convw = AR.alloc(96, F32, "p (a b) -> p a b", a=8)
    hgb = AR.alloc(1024, F32)
    waug = AR.alloc(2048, F32)
    stats = AR.alloc(4 * 8 * 4, F32, "p (a b) -> p a b", a=4)
    ssqp = AR.alloc(32 * 4, F32)
    pst = AR.alloc(3 * 4 * 4, F32, "p (a b) -> p a b", a=3)
    pstb = Buf("pst")
    pT = AR.alloc(2 * 512 * 2, BF16, "p (k t) -> p k t", k=2)
    hstat = AR.alloc(4 * 16 * 4, F32, "p (a b) -> p a b", a=4)
    uh = AR.alloc(8 * 2 * 4, F32, "p (a b) -> p a b", a=8)
    dec = AR.alloc(4 * 8 * 4, F32, "p (a b) -> p a b", a=4)
    hTh = AR.alloc(16 * 2 * 2, BF16, "p (k t) -> p k t", k=16)
    hThb = Buf("hTh")
    X = AR.alloc(4 * D * 4, F32, "p (a b) -> p a b", a=4)
    S = AR.alloc(2 * 4 * 256 * 4, F32, "p (a h b) -> p a h b", a=2, h=4)
    NSLOT = 3
    slots = [AR.alloc(16384, BF16) for _ in range(NSLOT)]
    slotbufs = [Buf(f"ws{i}") for i in range(NSLOT)]
    slot_i = [0]
    BASE = AR.mark()

    cb = Buf("consts")
    xb = [Buf(f"x{i}") for i in range(4)]
    stb = [Buf(f"st{i}") for i in range(4)]
    hstb = [Buf(f"hst{i}") for i in range(4)]
    ssqpb = Buf("ssqp")
    uhb = Buf("uh")
    decb = [Buf(f"dec{h}") for h in range(4)]
    Sb = [[Buf(f"S{a}{h}") for h in range(4)] for a in range(2)]

    def wload(pieces):
        i = slot_i[0] % NSLOT
        slot_i[0] += 1
        flat, buf = slots[i], slotbufs[i]
        for j, (dstf, src) in enumerate(pieces):
            P.dma("pool", f"w{i}", lambda e, d=dstf(flat), s=src: e.dma_start(out=d, in_=s),
                  writes=[buf] if j == 0 else [], partw=[] if j == 0 else [buf])
        return flat, buf

    def v3(flat, k, c):
        return flat[:, 0:k * c].rearrange("p (k c) -> p k c", k=k)

    for dst, src in [(ident, ident_d), (mfr, mfr_d.rearrange("p (a b) -> p a b", a=2)),
                     (lmm, lm_d.rearrange("p (a b) -> p a b", a=2)),
                     (gcol, gcol_d.rearrange("p (a b) -> p a b", a=3)),
                     (convw, convw_d.rearrange("p (a b) -> p a b", a=8)),
                     (waug[0:33, :], waug_d)]:
        P.dma("sp", "ldc", lambda e, d=dst, s=src: e.dma_start(out=d, in_=s), partw=[cb])
    P.dma("sp", "ldc", lambda e: e.dma_start(out=hgb, in_=hg_d[0].partition_broadcast(128)), partw=[cb])
    for a in range(2):
        for h in range(4):
            P.emit("dve", lambda e, a=a, h=h: e.memset(S[:, a, h, :], 0.0), writes=[Sb[a][h]])
    P.emit("dve", lambda e: e.memset(uh, 0.0), writes=[uhb])

    def mm(out_ap, bankbuf, pairs, reads, new=True):
        n = len(pairs)
        for j, (l, r) in enumerate(pairs):
            P.emit("pe", lambda e, l=l, r=r, j=j: e.matmul(out_ap, lhsT=l, rhs=r, start=(j == 0), stop=(j == n - 1)),
                   reads=reads if j == 0 else (), writes=[bankbuf] if (j == 0 and new) else (),
                   partw=[bankbuf] if (j == 0 and not new) else (), inc=(j == n - 1))

    def mm_fm(bk, bankbuf, wv, hT, hTb, wb, nk, per_tile):
        if not per_tile:
            mm(bk[:, :], bankbuf, [(wv[:, k, :], hT[:, k, :]) for k in range(nk)], [wb, hTb])
            return
        for i in range(4):
            ts = slice(i * 128, (i + 1) * 128)
            mm(bk[:, ts], bankbuf, [(wv[:, k, :], hT[:, k, ts]) for k in range(nk)], [wb, hTb[i]], new=(i == 0))

    def rstd_ops(st_ap, sbuf, n):
        P.emit("act", lambda e: e.activation(out=st_ap[:, 1:2], in_=st_ap[:, 0:1], func=AF.Ln, scale=1.0 / n, bias=EPS),
               reads=[sbuf], writes=[sbuf])
        P.emit("act", lambda e: e.activation(out=st_ap[:, 2:3], in_=st_ap[:, 1:2], func=AF.Exp, scale=-0.5),
               reads=[sbuf], writes=[sbuf])

    def stage_norm(gi, hT, hTb):
        m0 = AR.mark()
        xn = [AR.alloc(4096, BF16) for _ in range(2)]
        xnb = [Buf("xn0"), Buf("xn1")]
        junk = AR.alloc(4096, BF16)
        jb = Buf("junk")
        for i in range(4):
            st = stats[:, i, :]
            P.emit("act", lambda e, i=i, st=st: e.activation(out=junk, in_=X[:, i, :], func=AF.Square, accum_out=st[:, 0:1]),
                   reads=[xb[i]], writes=[jb, stb[i]])
            rstd_ops(st, stb[i], D)
        pend = {}

        def do_xn(i):
            st = stats[:, i, :]
            P.emit("dve", lambda e, i=i, st=st: e.tensor_scalar(out=xn[i % 2], in0=X[:, i, :], scalar1=st[:, 2:3], scalar2=None, op0=ALU.mult),
                   reads=[xb[i], stb[i]], writes=[xnb[i % 2]])
            pend[i] = []
            for half in range(2):
                bk, bb = nb()
                ptv = bk[:, :].bitcast(BF16)
                for j in range(8):
                    kc = half * 8 + j
                    P.emit("pe", lambda e, i=i, kc=kc, j=j, ptv=ptv: e.transpose(out=ptv[:, j * 128:(j + 1) * 128], in_=xn[i % 2][:, kc * 128:(kc + 1) * 128], identity=ident),
                           reads=[xnb[i % 2], cb] if j == 0 else (), writes=[bb] if j == 0 else (), inc=(j == 7))
                pend[i].append((ptv, bb, half))

        def do_ev(i):
            for ptv, bb, half in pend[i]:
                P.emit("dve", lambda e, i=i, half=half, ptv=ptv: e.tensor_tensor(
                    out=hT[:, half * 8:(half + 1) * 8, i * 128:(i + 1) * 128], in0=ptv.rearrange("p (k t) -> p k t", k=8),
                    in1=gcol[:, gi, half * 8:(half + 1) * 8].unsqueeze(2).to_broadcast([128, 8, 128]), op=ALU.mult),
                    reads=[bb, cb], partw=[hTb[i]])

        do_xn(0)
        do_xn(1)
        do_ev(0)
        do_xn(2)
        do_ev(1)
        do_xn(3)
        do_ev(2)
        do_ev(3)
        AR.reset(m0)

    def load_x(src_d, row0):
        for i in range(4):
            P.dma("sp", f"ldx{i}", lambda e, i=i: e.dma_start(out=X[:, i, :], in_=src_d[row0 + i * 128: row0 + (i + 1) * 128, :]),
                  writes=[xb[i]])

    def post_norm_tile(i, m_sb, msb_bufs, nblk, last_store_row0=None, next_load=None):
        st = stats[:, i, :]
        P.emit("dve", lambda e: e.reduce_sum(out=st[:, 0:1], in_=ssqp[:, i * 8:i * 8 + nblk], axis=mybir.AxisListType.X),
               reads=[ssqpb], writes=[stb[i]])
        rstd_ops(st, stb[i], D)
        if last_store_row0 is None:
            P.emit("dve", lambda e: e.scalar_tensor_tensor(out=X[:, i, :], in0=m_sb[:, i, :], scalar=st[:, 2:3], in1=X[:, i, :], op0=ALU.mult, op1=ALU.add),
                   reads=[msb_bufs[i], stb[i], xb[i]], writes=[xb[i]])
        else:
            P.emit("dve", lambda e: e.scalar_tensor_tensor(out=m_sb[:, i, :], in0=m_sb[:, i, :], scalar=st[:, 2:3], in1=X[:, i, :], op0=ALU.mult, op1=ALU.add),
                   reads=[stb[i], xb[i]], writes=[msb_bufs[i]])
            P.dma("sp", f"stx{i}", lambda e: e.dma_start(out=out_d[last_store_row0 + i * 128: last_store_row0 + (i + 1) * 128, :], in_=m_sb[:, i, :]),
                  reads=[msb_bufs[i]])
            if next_load is not None:
                src_d, r0 = next_load
                P.dma("sp", f"ldx{i}", lambda e: e.dma_start(out=X[:, i, :], in_=src_d[r0 + i * 128: r0 + (i + 1) * 128, :]),
                      writes=[xb[i]])

    def load_gb(j):
        gb = AR.alloc(8192, F32)
        gbb = Buf("gb")
        P.dma("sp", "ldg", lambda e: e.dma_start(out=gb, in_=gpost_d[j].partition_broadcast(128)), writes=[gbb])
        return gb, gbb

    def gla_g0(hT, hTb, own):
        g = {}
        g["sp"] = AR.alloc(8192, F32, "p (i c) -> p i c", i=4)
        g["spb"] = [Buf(f"sp{i}") for i in range(4)]
        g["kdec"] = AR.alloc(4096, BF16, "p (i c) -> p i c", i=4)
        g["kdecb"] = [Buf(f"kdec{i}") for i in range(4)]
        g["v"] = AR.alloc(8192, BF16, "p (i c) -> p i c", i=4)
        g["vb"] = [Buf(f"v{i}") for i in range(4)]
        m0 = AR.mark()
        e4 = AR.alloc(8192, F32, "p (i c) -> p i c", i=4)
        e4b = [Buf(f"e4{i}") for i in range(4)]
        aug = AR.alloc(2048, F32)
        augb = Buf("aug")
        etmp = AR.alloc(2048, F32)
        etb = Buf("etmp")
        flat, wb = wload([(lambda f: v3(f, 16, 16), w_in[:, :, C_ALR:C_ALR + 16])])
        wv = v3(flat, 16, 16)
        P.emit("dve", lambda e: e.memset(aug[0:33, :], 0.0), writes=[augb])
        P.emit("dve", lambda e: e.memset(aug[32:33, :], 1.0), writes=[augb])
        bk, bb = nb()
        mm(bk[0:16, :], bb, [(wv[:, k, :], hT[:, k, :]) for k in range(16)], [wb, hTb])
        P.emit("act", lambda e, bk=bk: e.activation(out=aug[0:16, :], in_=bk[0:16, :], func=AF.Copy), reads=[bb], writes=[augb])
        zb = []
        for i in range(4):
            bk, bb = nb()
            mm(bk[:, :], bb, [(aug[0:33, i * 128:(i + 1) * 128], waug[0:33, :])], [augb, cb])
            zb.append((bk, bb))
        for i in range(4):
            bk, bb = zb[i]
            P.emit("act", lambda e, bk=bk, i=i: e.activation(out=g["sp"][:, i, :], in_=bk[:, :], func=AF.Exp, scale=-1.0), reads=[bb], writes=[g["spb"][i]])
        for i in range(4):
            P.emit("act", lambda e, i=i: e.activation(out=g["sp"][:, i, :], in_=g["sp"][:, i, :], func=AF.Ln, bias=1.0), reads=[g["spb"][i]], writes=[g["spb"][i]])
        zb = []
        for i in range(4):
            bk, bb = nb()
            mm(bk[:, :], bb, [(lmm[:, 1, :], g["sp"][:, i, :])], [cb, g["spb"][i]])
            zb.append((bk, bb))
        for i in range(4):
            bk, bb = zb[i]
            P.emit("act", lambda e, bk=bk, i=i: e.activation(out=e4[:, i, :], in_=bk[:, :], func=AF.Exp), reads=[bb], writes=[e4b[i]])
        flat, wb = wload([(lambda f: v3(f, 16, 512), w_in[:, :, C_K:C_K + 512])])
        wv = v3(flat, 16, 512)
        for i in range(4):
            bk, bb = nb()
            mm(bk[:, :], bb, [(hT[:, k, i * 128:(i + 1) * 128], wv[:, k, :]) for k in range(16)], [wb, hTb[i]])
            P.emit("dve", lambda e, bk=bk, i=i: e.tensor_tensor(out=g["kdec"][:, i, :], in0=bk[:, :], in1=e4[:, i, :], op=ALU.mult),
                   reads=[bb, e4b[i]], writes=[g["kdecb"][i]])
        for hv in range(2):
            flat, wb = wload([(lambda f: v3(f, 16, 512), w_in[:, :, C_V + hv * 512:C_V + (hv + 1) * 512])])
            wv = v3(flat, 16, 512)
            for i in range(4):
                bk, bb = nb()
                mm(bk[:, :], bb, [(hT[:, k, i * 128:(i + 1) * 128], wv[:, k, :]) for k in range(16)], [wb, hTb[i]])
                P.emit("act", lambda e, bk=bk, i=i, hv=hv: e.activation(out=g["v"][:, i, hv * 512:(hv + 1) * 512], in_=bk[:, :], func=AF.Copy),
                       reads=[bb], writes=[g["vb"][i]] if hv == 0 else (), partw=[g["vb"][i]] if hv == 1 else ())
        AR.reset(m0)
        return g

    def gla_bT(g, h, bTs, bTsb, with_dec=True):
        bk, bb = nb()
        for i in range(4):
            P.emit("pe", lambda e, bk=bk, i=i: e.matmul(bk[:, i * 128:(i + 1) * 128], lhsT=g["sp"][:, i, h * 128:(h + 1) * 128], rhs=lmm[:, 0, :], start=True, stop=True),
                   reads=[g["spb"][i], cb], writes=[bb] if i == 0 else (), partw=[bb] if i > 0 else (), inc=(i == 3))
        P.emit("act", lambda e, bk=bk: e.activation(out=bTs, in_=bk[:, :], func=AF.Copy), reads=[bb], writes=[bTsb])
        if with_dec:
            P.emit("act", lambda e: e.activation(out=dec[:, h, :], in_=bTs.rearrange("p (c t) -> p c t", c=8)[:, :, 63], func=AF.Exp),
                   reads=[bTsb], writes=[decb[h]])

    def gla_recur(g, h, cur, sbf=None, sbfb=None):
        for c in range(8):
            i, hf = c // 2, c % 2
            bk, bb = nb()
            mm(bk[:, 0:256], bb, [(g["kdec"][hf * 64:(hf + 1) * 64, i, h * 128:(h + 1) * 128], g["v"][hf * 64:(hf + 1) * 64, i, h * 256:(h + 1) * 256])],
               [g["kdecb"][i], g["vb"][i]])
            nx = 1 - cur
            P.emit("dve", lambda e, bk=bk, c=c, cur=cur, nx=nx: e.scalar_tensor_tensor(out=S[:, nx, h, :], in0=S[:, cur, h, :], scalar=dec[:, h, c:c + 1], in1=bk[:, 0:256], op0=ALU.mult, op1=ALU.add),
                   reads=[bb, Sb[cur][h], decb[h]], writes=[Sb[nx][h]])
            cur = nx
            if sbf is not None and c < 7:
                P.emit("act", lambda e, c=c, cur=cur: e.activation(out=sbf[:, c + 1, :], in_=S[:, cur, h, :], func=AF.Copy),
                       reads=[Sb[cur][h]], writes=[sbfb[c + 1]])
        return cur

    def gla_recur_all(g, sbfs=None, sbfbs=None):
        cur = 0
        for c in range(8):
            i, hf = c // 2, c % 2
            nx = 1 - cur
            bks = []
            for h in range(4):
                bk, bb = nb()
                mm(bk[:, 0:256], bb, [(g["kdec"][hf * 64:(hf + 1) * 64, i, h * 128:(h + 1) * 128], g["v"][hf * 64:(hf + 1) * 64, i, h * 256:(h + 1) * 256])],
                   [g["kdecb"][i], g["vb"][i]])
                bks.append((bk, bb))
            for h in range(4):
                bk, bb = bks[h]
                P.emit("dve", lambda e, bk=bk, c=c, cur=cur, nx=nx, h=h: e.scalar_tensor_tensor(out=S[:, nx, h, :], in0=S[:, cur, h, :], scalar=dec[:, h, c:c + 1], in1=bk[:, 0:256], op0=ALU.mult, op1=ALU.add),
                       reads=[bb, Sb[cur][h], decb[h]], writes=[Sb[nx][h]])
            if sbfs is not None and c < 7:
                for h in range(4):
                    P.emit("act", lambda e, c=c, nx=nx, h=h: e.activation(out=sbfs[h][:, c + 1, :], in_=S[:, nx, h, :], func=AF.Copy),
                           reads=[Sb[nx][h]], writes=[sbfbs[h][c + 1]])
            cur = nx
        return cur

    xloaded = [False]
    have_prefix = n_pre_pass > 0

    def prefix_pass(pi, last):
        P.barrier()
        AR.reset(BASE)
        if not xloaded[0]:
            load_x(xp_d, pi * TP)
        hT = AR.alloc(16384, BF16, "p (k t) -> p k t", k=16)
        hTb = [Buf(f"hT_{i}") for i in range(4)]
        stage_norm(0, hT, hTb)
        if last:
            load_x(x_d, 0)
        else:
            load_x(xp_d, (pi + 1) * TP)
        xloaded[0] = True
        g = gla_g0(hT, hTb, own=False)
        bTs = AR.alloc(2048, F32)
        bTsb = Buf("bTs")
        for h in range(4):
            gla_bT(g, h, bTs, bTsb)
        gla_recur_all(g)
        if last:
            P.emit("dve", lambda e: e.tensor_copy(out=hTh, in_=hT[:, :, 510:512]), reads=[hTb], writes=[hThb])

    def own_pass(pi, last):
        row0 = pi * TP
        dump = dump_all if pi == 0 else (lambda *a: None)
        P.barrier()
        if pi > 0:
            for en in ("act", "dve", "pe"):
                P.final_wait(en, ["stx0", "stx1"])
        AR.reset(BASE)
        if not xloaded[0]:
            load_x(x_d, row0)
        xloaded[0] = True
        hT = AR.alloc(16384, BF16, "p (k t) -> p k t", k=16)
        hTb = [Buf(f"hT_{i}") for i in range(4)]
        stage_norm(0, hT, hTb)
        dump("hT", hT, [hTb])
        if stop == "A":
            return

        P.barrier(engines=("sp",))
        if pi > 0:
            for en in ("act", "dve", "pe", "sp"):
                P.final_wait(en, ["stx2", "stx3"])
        ua = AR.alloc(8192, BF16, "p (k t) -> p k t", k=8)
        uab = Buf("ua")
        mB1 = AR.mark()
        pf = [AR.alloc(1024, F32) for _ in range(2)]
        pfb = [Buf("pf0"), Buf("pf1")]
        pbf = [AR.alloc(512, BF16) for _ in range(2)]
        pbfb = [Buf("pbf0"), Buf("pbf1")]
        ux = [AR.alloc(2048, F32) for _ in range(2)]
        uxb = [Buf("ux0"), Buf("ux1")]
        u = [AR.alloc(514 * 4, F32) for _ in range(2)]
        ub_ = [Buf("u0"), Buf("u1")]
        y = [AR.alloc(2048, F32) for _ in range(2)]
        yb = [Buf("y0"), Buf("y1")]
        halo_t = [AR.alloc(64, F32)[:, 0:2] for _ in range(2)]
        halo_tb = [Buf("halo_t0"), Buf("halo_t1")]
        for cc in range(8):
            q = cc % 2
            pieces = []
            for j, c0 in enumerate((C_AX, C_AB, C_AC)):
                pieces.append((lambda f, j=j: f[:, j * 2048:(j + 1) * 2048].rearrange("p (k c) -> p k c", k=16), w_in[:, :, c0 + cc * 128:c0 + (cc + 1) * 128]))
            flat, wb = wload(pieces)
            wx, wbb, wc = [flat[:, j * 2048:(j + 1) * 2048].rearrange("p (k c) -> p k c", k=16) for j in range(3)]
            bkx, bbx = nb()
            bkc, bbc = nb()
            bkb, bbb = nb()
            if cc == 0:
                for i_ in range(4):
                    ts_ = slice(i_ * 128, (i_ + 1) * 128)
                    for bk_, bb_, w_ in ((bkx, bbx, wx), (bkc, bbc, wc), (bkb, bbb, wbb)):
                        mm(bk_[:, ts_], bb_, [(w_[:, k, :], hT[:, k, ts_]) for k in range(16)], [wb, hTb[i_]], new=(i_ == 0))
            else:
                mm(bkx[:, :], bbx, [(wx[:, k, :], hT[:, k, :]) for k in range(16)], [wb, hTb])
                mm(bkc[:, :], bbc, [(wc[:, k, :], hT[:, k, :]) for k in range(16)], [wb, hTb])
                mm(bkb[:, :], bbb, [(wbb[:, k, :], hT[:, k, :]) for k in range(16)], [wb, hTb])
            P.emit("act", lambda e, bkx=bkx, q=q: e.activation(out=ux[q], in_=bkx[:, :], func=AF.Copy), reads=[bbx], writes=[uxb[q]])
            if pi == 0 and have_prefix:
                bkh, bbh = nb()
                mm(bkh[:, 0:2], bbh, [(wx[:, k, :], hTh[:, k, :]) for k in range(16)], [wb, hThb])
                mm(bkh[:, 2:4], bbh, [(wc[:, k, :], hTh[:, k, :]) for k in range(16)], [wb, hThb], new=False)
                P.emit("act", lambda e, bkh=bkh, q=q: e.activation(out=halo_t[q], in_=bkh[:, 0:2], func=AF.Copy), reads=[bbh], writes=[halo_tb[q]])
                P.emit("dve", lambda e, bkh=bkh, q=q: e.tensor_tensor(out=u[q][:, 0:2], in0=bkh[:, 2:4], in1=halo_t[q], op=ALU.mult),
                       reads=[bbh, halo_tb[q]], writes=[ub_[q]])
            else:
                P.emit("dve", lambda e, q=q, cc=cc: e.tensor_copy(out=u[q][:, 0:2], in_=uh[:, cc, :]), reads=[uhb], writes=[ub_[q]])
            P.emit("dve", lambda e, q=q, bkc=bkc: e.tensor_tensor(out=u[q][:, 2:514], in0=bkc[:, :], in1=ux[q], op=ALU.mult),
                   reads=[bbc, uxb[q]], partw=[ub_[q]])
            P.emit("dve", lambda e, q=q, cc=cc: e.tensor_scalar(out=y[q], in0=u[q][:, 0:512], scalar1=convw[:, cc, 0:1], scalar2=None, op0=ALU.mult),
                   reads=[ub_[q], cb], writes=[yb[q]])
            for j in (1, 2):
                P.emit("dve", lambda e, q=q, cc=cc, j=j: e.scalar_tensor_tensor(out=y[q], in0=u[q][:, j:j + 512], scalar=convw[:, cc, j:j + 1], in1=y[q], op0=ALU.mult, op1=ALU.add),
                       reads=[ub_[q], cb, yb[q]], writes=[yb[q]])
            P.emit("dve", lambda e, q=q, cc=cc, bkb=bkb: e.tensor_tensor(out=ua[:, cc, :], in0=bkb[:, :], in1=y[q], op=ALU.mult),
                   reads=[bbb, yb[q]], partw=[uab])
            P.emit("dve", lambda e, q=q, cc=cc: e.tensor_copy(out=uh[:, cc, :], in_=u[q][:, 512:514]), reads=[ub_[q]], writes=[uhb])
        pTb = Buf("pT")
        for i in range(4):
            q = i % 2
            P.dma("sp", f"ldp{q}", lambda e, i=i, q=q: e.dma_start(out=pf[q], in_=p_d[row0 + i * 128: row0 + (i + 1) * 128, :]), writes=[pfb[q]])
            P.emit("dve", lambda e, q=q: e.tensor_copy(out=pbf[q], in_=pf[q]), reads=[pfb[q]], writes=[pbfb[q]])
            bk, bb = nb()
            ptv = bk[:, :].bitcast(BF16)
            for j in range(2):
                P.emit("pe", lambda e, q=q, j=j, ptv=ptv: e.transpose(out=ptv[:, j * 128:(j + 1) * 128], in_=pbf[q][:, j * 128:(j + 1) * 128], identity=ident),
                       reads=[pbfb[q], cb] if j == 0 else (), writes=[bb] if j == 0 else (), inc=(j == 1))
            P.emit("act", lambda e, i=i, ptv=ptv: e.activation(out=pT[:, :, i * 128:(i + 1) * 128], in_=ptv[:, 0:256].rearrange("p (k t) -> p k t", k=2), func=AF.Copy),
                   reads=[bb], partw=[pTb])
        dump("ua", ua, [uab])
        if stop == "B1":
            return

        P.barrier()
        AR.reset(mB1)
        ubT = AR.alloc(8192, BF16, "p (k t) -> p k t", k=8)
        ubTb = Buf("ubT")
        mB2 = AR.mark()
        g = gla_g0(hT, hTb, own=True)
        if stop == "B2a":
            dump("sp", g["sp"], g["spb"])
            dump("kdec", g["kdec"], g["kdecb"])
            dump("v", g["v"], g["vb"])
            return
        ubt = AR.alloc(8192, BF16, "p (i c) -> p i c", i=4)
        ubtb = [Buf(f"ubt{i}") for i in range(4)]
        bTs = AR.alloc(2048, F32)
        bTsb = Buf("bTs")
        E = [AR.alloc(2048, F32) for _ in range(3)]
        Eb = [Buf(f"E{j}") for j in range(3)]
        QK = [[AR.alloc(1024, BF16) for _ in range(6)] for _ in range(2)]
        QKb = [[Buf(f"qk{s_}{j}") for j in range(6)] for s_ in range(2)]
        SBF = [AR.alloc(8 * 256 * 2, BF16, "p (c v) -> p c v", c=8) for _ in range(4)]
        SBFb = [[Buf(f"sbf{s_}{c}") for c in range(8)] for s_ in range(4)]
        SO = [AR.alloc(4 * 256 * 4, F32, "p (i v) -> p i v", i=4) for _ in range(2)]
        SOb = [[Buf(f"so{s_}{i}") for i in range(4)] for s_ in range(2)]
        atmp = [AR.alloc(512, BF16, "p (a b) -> p a b", a=2) for _ in range(2)]
        atb = [Buf("atmp0"), Buf("atmp1")]
        attT = [AR.alloc(256, BF16) for _ in range(2)]
        attb = [Buf("attT0"), Buf("attT1")]
        on = AR.alloc(1024, F32)
        onb = Buf("on")
        sg = AR.alloc(1024, F32)
        sgb = Buf("sg")
        junk2 = AR.alloc(512, BF16)
        jb2 = Buf("junk2")
        for s_ in range(2):
            P.emit("dve", lambda e, s_=s_: e.memset(QK[s_][4], 0.0), writes=[QKb[s_][4]])
            P.emit("dve", lambda e, s_=s_: e.memset(QK[s_][5], 0.0), writes=[QKb[s_][5]])

        def gla_part1(h):
            s_ = h % 2
            qf, qr, kf, kr, qbA, qbB = QK[s_]
            qkb = QKb[s_]
            gla_bT(g, h, bTs, bTsb, with_dec=False)
            bv = bTs.rearrange("p (c t) -> p c t", c=8)
            P.emit("dve", lambda e, bv=bv: e.tensor_tensor(out=E[0].rearrange("p (c t) -> p c t", c=8), in0=bv, in1=bv[:, :, 32:33].to_broadcast([128, 8, 64]), op=ALU.subtract),
                   reads=[bTsb], writes=[Eb[0]])
            P.emit("act", lambda e: e.activation(out=E[1], in_=E[0], func=AF.Exp, scale=-1.0), reads=[Eb[0]], writes=[Eb[1]])
            P.emit("act", lambda e: e.activation(out=E[0], in_=E[0], func=AF.Exp), reads=[Eb[0]], writes=[Eb[0]])
            P.emit("act", lambda e: e.activation(out=E[2], in_=bTs, func=AF.Exp), reads=[bTsb], writes=[Eb[2]])
            flat, wb = wload([(lambda f: f[:, 0:2048].rearrange("p (k c) -> p k c", k=16), w_in[:, :, C_Q + h * 128:C_Q + (h + 1) * 128]),
                              (lambda f: f[:, 2048:4096].rearrange("p (k c) -> p k c", k=16), w_in[:, :, C_K + h * 128:C_K + (h + 1) * 128])])
            wq = flat[:, 0:2048].rearrange("p (k c) -> p k c", k=16)
            wk = flat[:, 2048:4096].rearrange("p (k c) -> p k c", k=16)
            bkq, bbq = nb()
            mm(bkq[:, :], bbq, [(wq[:, k, :], hT[:, k, :]) for k in range(16)], [wb, hTb])
            bkk, bbk = nb()
            mm(bkk[:, :], bbk, [(wk[:, k, :], hT[:, k, :]) for k in range(16)], [wb, hTb])
            P.emit("dve", lambda e, bkq=bkq: e.scalar_tensor_tensor(out=qf, in0=bkq[:, :], scalar=QSCALE, in1=E[0], op0=ALU.mult, op1=ALU.mult),
                   reads=[bbq, Eb[0]], writes=[qkb[0]])
            P.emit("dve", lambda e, bkq=bkq: e.scalar_tensor_tensor(out=qr, in0=bkq[:, :], scalar=QSCALE, in1=E[1], op0=ALU.mult, op1=ALU.mult),
                   reads=[bbq, Eb[1]], writes=[qkb[1]])
            for par, qb_, qi in ((0, qbA, 4), (1, qbB, 5)):
                P.emit("dve", lambda e, bkq=bkq, par=par, qb_=qb_: e.scalar_tensor_tensor(
                    out=qb_.rearrange("p (i a t) -> p i a t", i=4, a=2)[:, :, par, :],
                    in0=bkq[:, :].rearrange("p (i a t) -> p i a t", i=4, a=2)[:, :, par, :], scalar=QSCALE,
                    in1=E[2].rearrange("p (i a t) -> p i a t", i=4, a=2)[:, :, par, :], op0=ALU.mult, op1=ALU.mult),
                    reads=[bbq, Eb[2]], writes=[qkb[qi]])
            P.emit("dve", lambda e, bkk=bkk: e.tensor_tensor(out=kf, in0=bkk[:, :], in1=E[1], op=ALU.mult), reads=[bbk, Eb[1]], writes=[qkb[2]])
            P.emit("dve", lambda e, bkk=bkk: e.tensor_tensor(out=kr, in0=bkk[:, :], in1=E[0], op=ALU.mult), reads=[bbk, Eb[0]], writes=[qkb[3]])

        def gla_part2(h):
            s_ = h % 2
            qf, qr, kf, kr, qbA, qbB = QK[s_]
            qkb = QKb[s_]
            sbf, sbfb = SBF[h], SBFb[h]
            flat_o, wob = wload([(lambda f: v3(f, 16, 256), w_in[:, :, C_OG + h * 256:C_OG + (h + 1) * 256])])
            wog = v3(flat_o, 16, 256)

            def og_stage(i):
                ts = slice(i * 128, (i + 1) * 128)
                bkg, bbg = nb()
                mm(bkg[:, 0:256], bbg, [(hT[:, k, ts], wog[:, k, :]) for k in range(16)], [wob, hTb[i]])
                P.emit("act", lambda e, bkg=bkg: e.activation(out=sg, in_=bkg[:, 0:256], func=AF.Exp, scale=-1.0), reads=[bbg], writes=[sgb])
                P.emit("act", lambda e: e.activation(out=sg, in_=sg, func=AF.Ln, bias=1.0), reads=[sgb], writes=[sgb])
                P.emit("act", lambda e: e.activation(out=sg, in_=sg, func=AF.Exp, scale=-1.0), reads=[sgb], writes=[sgb])
                P.emit("dve", lambda e, bkg=bkg, i=i, s_=s_: e.tensor_tensor(out=SO[s_][:, i, :], in0=bkg[:, 0:256], in1=sg, op=ALU.mult),
                       reads=[bbg, sgb], writes=[SOb[s_][i]])

            def att_stage(i):
                ts = slice(i * 128, (i + 1) * 128)
                a_ = i % 2
                bk, bb = nb()
                mm(bk[:, 0:128], bb, [(kf[:, ts], qf[:, ts])], [qkb[2], qkb[0]])
                mm(bk[:, 128:256], bb, [(kr[:, ts], qr[:, ts])], [qkb[3], qkb[1]], new=False)
                P.emit("dve", lambda e, bk=bk, a_=a_: e.tensor_tensor(out=atmp[a_], in0=bk[:, 0:256].rearrange("p (a b) -> p a b", a=2), in1=mfr, op=ALU.mult),
                       reads=[bb, cb], writes=[atb[a_]])
                P.emit("dve", lambda e, a_=a_: e.tensor_tensor(out=attT[a_], in0=atmp[a_][:, 0, :], in1=atmp[a_][:, 1, :], op=ALU.add), reads=[atb[a_]], writes=[attb[a_]])

            def out_stage(i):
                ts = slice(i * 128, (i + 1) * 128)
                a_ = i % 2
                bko, bbo = nb()
                mm(bko[:, 0:256], bbo, [(attT[a_], g["v"][:, i, h * 256:(h + 1) * 256]), (qbA[:, ts], sbf[:, 2 * i, :]), (qbB[:, ts], sbf[:, 2 * i + 1, :])],
                   [attb[a_], g["vb"][i], qkb[4], qkb[5], sbfb[2 * i], sbfb[2 * i + 1]])
                hs = hstat[:, i, h * 4:(h + 1) * 4]
                P.emit("act", lambda e, bko=bko, hs=hs: e.activation(out=junk2, in_=bko[:, 0:256], func=AF.Square, accum_out=hs[:, 0:1]),
                       reads=[bbo], writes=[jb2, hstb[i]])
                rstd_ops(hs, hstb[i], 256)
                P.emit("dve", lambda e, bko=bko, hs=hs: e.scalar_tensor_tensor(out=on, in0=bko[:, 0:256], scalar=hs[:, 2:3], in1=hgb, op0=ALU.mult, op1=ALU.mult),
                       reads=[bbo, hstb[i], cb], writes=[onb])
                P.emit("dve", lambda e, i=i, s_=s_: e.tensor_tensor(out=ubt[:, i, h * 256:(h + 1) * 256], in0=on, in1=SO[s_][:, i, :], op=ALU.mult),
                       reads=[onb, SOb[s_][i]], partw=[ubtb[i]])

            att_stage(0)
            att_stage(1)
            og_stage(0)
            out_stage(0)
            att_stage(2)
            og_stage(1)
            out_stage(1)
            att_stage(3)
            og_stage(2)
            out_stage(2)
            og_stage(3)
            out_stage(3)

        for h_ in range(4):
            gla_bT(g, h_, bTs, bTsb)
            P.emit("act", lambda e, h_=h_: e.activation(out=SBF[h_][:, 0, :], in_=S[:, 0, h_, :], func=AF.Copy), reads=[Sb[0][h_]], writes=[SBFb[h_][0]])
        gla_recur_all(g, SBF, SBFb)
        gla_part1(0)
        gla_part1(1)
        gla_part2(0)
        gla_part1(2)
        gla_part2(1)
        gla_part1(3)
        gla_part2(2)
        gla_part2(3)
        for i in range(4):
            bk, bb = nb()
            ptv = bk[:, :].bitcast(BF16)
            for j in range(8):
                P.emit("pe", lambda e, i=i, j=j, ptv=ptv: e.transpose(out=ptv[:, j * 128:(j + 1) * 128], in_=ubt[:, i, j * 128:(j + 1) * 128], identity=ident),
                       reads=[ubtb[i], cb] if j == 0 else (), writes=[bb] if j == 0 else (), inc=(j == 7))
            P.emit("act", lambda e, i=i, ptv=ptv: e.activation(out=ubT[:, :, i * 128:(i + 1) * 128], in_=ptv.rearrange("p (k t) -> p k t", k=8), func=AF.Copy),
                   reads=[bb], partw=[ubTb])
        dump("ubT", ubT, [ubTb])
        if stop == "B2":
            return

        AR.reset(mB2)
        mixT = AR.alloc(16384, BF16, "p (k t) -> p k t", k=16)
        mixb = Buf("mixT")
        mB3 = AR.mark()
        sga = [AR.alloc(2048, F32) for _ in range(2)]
        sgab = [Buf("sga0"), Buf("sga1")]
        sgb2 = [AR.alloc(2048, F32) for _ in range(2)]
        sgbb = [Buf("sgb0"), Buf("sgb1")]
        for dc in range(16):
            q = dc % 2
            cs = slice(dc * 128, (dc + 1) * 128)
            pieces = [(lambda f: f[:, 0:1024].rearrange("p (k c) -> p k c", k=8), w_a_out[:, :, cs]),
                      (lambda f: f[:, 1024:3072].rearrange("p (k c) -> p k c", k=16), w_in[:, :, C_GA + dc * 128:C_GA + (dc + 1) * 128]),
                      (lambda f: f[:, 3072:4096].rearrange("p (k c) -> p k c", k=8), w_b_out[:, :, cs]),
                      (lambda f: f[:, 4096:6144].rearrange("p (k c) -> p k c", k=16), w_in[:, :, C_GB + dc * 128:C_GB + (dc + 1) * 128])]
            flat, wb = wload(pieces)
            wa = flat[:, 0:1024].rearrange("p (k c) -> p k c", k=8)
            wga = flat[:, 1024:3072].rearrange("p (k c) -> p k c", k=16)
            wbo = flat[:, 3072:4096].rearrange("p (k c) -> p k c", k=8)
            wgb = flat[:, 4096:6144].rearrange("p (k c) -> p k c", k=16)
            bya, bbya = nb()
            mm(bya[:, :], bbya, [(wa[:, k, :], ua[:, k, :]) for k in range(8)], [wb, uab])
            bga, bbga = nb()
            mm(bga[:, :], bbga, [(wga[:, k, :], hT[:, k, :]) for k in range(16)], [wb, hTb])
            byb, bbyb = nb()
            mm(byb[:, :], bbyb, [(wbo[:, k, :], ubT[:, k, :]) for k in range(8)], [wb, ubTb])
            bgb, bbgb = nb()
            mm(bgb[:, :], bbgb, [(wgb[:, k, :], hT[:, k, :]) for k in range(16)], [wb, hTb])
            P.emit("act", lambda e, bga=bga, q=q: e.activation(out=sga[q], in_=bga[:, :], func=AF.Sigmoid), reads=[bbga], writes=[sgab[q]])
            P.emit("act", lambda e, bgb=bgb, q=q: e.activation(out=sgb2[q], in_=bgb[:, :], func=AF.Sigmoid), reads=[bbgb], writes=[sgbb[q]])
            P.emit("dve", lambda e, bya=bya, q=q: e.tensor_tensor(out=sga[q], in0=bya[:, :], in1=sga[q], op=ALU.mult), reads=[bbya, sgab[q]], writes=[sgab[q]])
            P.emit("dve", lambda e, byb=byb, q=q: e.tensor_tensor(out=sgb2[q], in0=byb[:, :], in1=sgb2[q], op=ALU.mult), reads=[bbyb, sgbb[q]], writes=[sgbb[q]])
            P.emit("dve", lambda e, q=q, dc=dc: e.tensor_tensor(out=mixT[:, dc, :], in0=sga[q], in1=sgb2[q], op=ALU.add), reads=[sgab[q], sgbb[q]], partw=[mixb])
        dump("mixT", mixT, [mixb])
        if stop == "B3":
            return

        P.barrier(engines=("sp",))
        AR.reset(mB3)
        m_sb = AR.alloc(4 * D * 4, F32, "p (i c) -> p i c", i=4)
        msb_bufs = [Buf(f"msb{i}") for i in range(4)]
        junk = AR.alloc(1024, BF16)
        jb = Buf("junk")
        gb, gbb = load_gb(0)

        def proj_tok(act, actb, nk_list, wsrc, m_sb, msb_bufs, junk, jb, gb, gbb):
            for cbk in range(4):
                cs = slice(cbk * 512, (cbk + 1) * 512)
                bks = [nb() for _ in range(4)]
                k0 = 0
                for part, nk in enumerate(nk_list):
                    flat, wb = wload([(lambda f, nk=nk: v3(f, nk, 512), wsrc[:, k0:k0 + nk, cs])])
                    wv = v3(flat, nk, 512)
                    for i in range(4):
                        bk, bb = bks[i]
                        for kk in range(nk):
                            first = (part == 0 and kk == 0)
                            lastk = (part == len(nk_list) - 1 and kk == nk - 1)
                            P.emit("pe", lambda e, bk=bk, i=i, kk=kk, k0=k0, first=first, lastk=lastk, wv=wv: e.matmul(
                                bk[:, :], lhsT=act[:, k0 + kk, i * 128:(i + 1) * 128], rhs=wv[:, kk, :], start=first, stop=lastk),
                                reads=[wb, actb] if kk == 0 else (), writes=[bb] if first else (), partw=[bb] if (kk == 0 and not first) else (),
                                inc=(kk == nk - 1))
                    k0 += nk
                for i in range(4):
                    bk, bb = bks[i]
                    P.emit("act", lambda e, bk=bk, i=i, cbk=cbk: e.activation(out=junk, in_=bk[:, :], func=AF.Square, accum_out=ssqp[:, i * 8 + cbk:i * 8 + cbk + 1]),
                           reads=[bb], writes=[jb], partw=[ssqpb])
                    P.emit("dve", lambda e, bk=bk, i=i, cs=cs: e.tensor_tensor(out=m_sb[:, i, cs], in0=bk[:, :], in1=gb[:, cs], op=ALU.mult),
                           reads=[bb, gbb], partw=[msb_bufs[i]])
                    if cbk == 3:
                        post_norm_tile(i, m_sb, msb_bufs, 4)

        proj_tok(mixT, mixb, [16], w_mix, m_sb, msb_bufs, junk, jb, gb, gbb)
        dump("x1", X, xb)
        if stop == "B4":
            return

        P.barrier()
        AR.reset(BASE)
        hid = AR.alloc(44 * 512 * 2, BF16, "p (k t) -> p k t", k=44)
        hidb = Buf("hid")
        mC = AR.mark()
        hT2 = AR.alloc(16384, BF16, "p (k t) -> p k t", k=16)
        hTb2 = [Buf(f"hT2_{i}") for i in range(4)]
        stage_norm(1, hT2, hTb2)
        sg2 = [AR.alloc(2048, F32) for _ in range(2)]
        sg2b = [Buf("sg0"), Buf("sg1")]
        for fb in range(22):
            cs = slice(fb * 256, (fb + 1) * 256)
            flat_w, wbw = wload([(lambda f: f[:, 0:4096].rearrange("p (k c) -> p k c", k=16), w_fg[:, :, cs]),
                                 (lambda f: f[:, 4096:8192].rearrange("p (k c) -> p k c", k=16), w_fu[:, :, cs])])
            wg_ = flat_w[:, 0:4096].rearrange("p (k c) -> p k c", k=16)
            wu_ = flat_w[:, 4096:8192].rearrange("p (k c) -> p k c", k=16)
            gus = [(nb(), nb()) for _ in range(2)]
            if fb == 0:
                for i_ in range(4):
                    ts_ = slice(i_ * 128, (i_ + 1) * 128)
                    for j in range(2):
                        (bg, bbg), (bu, bbu) = gus[j]
                        mm(bg[:, ts_], bbg, [(wg_[:, k, j * 128:(j + 1) * 128], hT2[:, k, ts_]) for k in range(16)], [wbw, hTb2[i_]], new=(i_ == 0))
                        mm(bu[:, ts_], bbu, [(wu_[:, k, j * 128:(j + 1) * 128], hT2[:, k, ts_]) for k in range(16)], [wbw, hTb2[i_]], new=(i_ == 0))
            for j in range(2):
                q = j % 2
                fc = fb * 2 + j
                (bg, bbg), (bu, bbu) = gus[j]
                if fb > 0:
                    mm(bg[:, :], bbg, [(wg_[:, k, j * 128:(j + 1) * 128], hT2[:, k, :]) for k in range(16)], [wbw, hTb2])
                    mm(bu[:, :], bbu, [(wu_[:, k, j * 128:(j + 1) * 128], hT2[:, k, :]) for k in range(16)], [wbw, hTb2])
                P.emit("act", lambda e, bg=bg, q=q: e.activation(out=sg2[q], in_=bg[:, :], func=AF.Silu), reads=[bbg], writes=[sg2b[q]])
                P.emit("dve", lambda e, bu=bu, q=q, fc=fc: e.tensor_tensor(out=hid[:, fc, :], in0=bu[:, :], in1=sg2[q], op=ALU.mult),
                       reads=[bbu, sg2b[q]], partw=[hidb])
        P.barrier()
        AR.reset(mC)
        m_sb2 = AR.alloc(4 * D * 4, F32, "p (i c) -> p i c", i=4)
        msb_bufs2 = [Buf(f"msb{i}") for i in range(4)]
        junk3 = AR.alloc(1024, BF16)
        jb3 = Buf("junk")
        gb2, gbb2 = load_gb(1)
        proj_tok(hid, hidb, [15, 15, 14], w_fd, m_sb2, msb_bufs2, junk3, jb3, gb2, gbb2)
        dump("x2", X, xb)
        if stop == "C":
            return

        P.barrier()
        AR.reset(BASE)
        hT3 = AR.alloc(16384, BF16, "p (k t) -> p k t", k=16)
        hTb3 = [Buf(f"hT3_{i}") for i in range(4)]
        m_sb3 = AR.alloc(4 * D * 4, F32, "p (i c) -> p i c", i=4)
        msb_bufs3 = [Buf(f"msb{i}") for i in range(4)]
        junk4 = AR.alloc(1024, BF16)
        jb4 = Buf("junk")
        gb3, gbb3 = load_gb(2)
        sgd = [AR.alloc(2048, F32) for _ in range(2)]
        sgdb = [Buf("sgd0"), Buf("sgd1")]
        mD = AR.mark()
        stage_norm(2, hT3, hTb3)
        nxt = (x_d, (pi + 1) * TP) if not last else None
        for cbk in range(8):
            cs = slice(cbk * 256, (cbk + 1) * 256)
            flat_w, wbw = wload([(lambda f: f[:, 0:4096].rearrange("p (k c) -> p k c", k=16), w_pg[:, :, cs]),
                                 (lambda f: f[:, 4096:4608].rearrange("p (k c) -> p k c", k=2), w_pp[:, :, cs])])
            wg_ = flat_w[:, 0:4096].rearrange("p (k c) -> p k c", k=16)
            wp_ = flat_w[:, 4096:4608].rearrange("p (k c) -> p k c", k=2)
            for i in range(4):
                q = i % 2
                ts = slice(i * 128, (i + 1) * 128)
                bg, bbg = nb()
                mm(bg[:, 0:256], bbg, [(hT3[:, k, ts], wg_[:, k, :]) for k in range(16)], [wbw, hTb3[i]])
                mm(bg[:, 256:512], bbg, [(pT[:, k, ts], wp_[:, k, :]) for k in range(2)], [wbw, pTb], new=False)
                P.emit("act", lambda e, bg=bg, q=q: e.activation(out=sgd[q][:, 0:256], in_=bg[:, 0:256], func=AF.Sigmoid), reads=[bbg], writes=[sgdb[q]])
                P.emit("dve", lambda e, bg=bg, q=q: e.tensor_tensor(out=sgd[q][:, 0:256], in0=bg[:, 256:512], in1=sgd[q][:, 0:256], op=ALU.mult),
                       reads=[bbg, sgdb[q]], writes=[sgdb[q]])
                P.emit("act", lambda e, q=q, i=i, cbk=cbk: e.activation(out=junk4[:, 0:256], in_=sgd[q][:, 0:256], func=AF.Square, accum_out=ssqp[:, i * 8 + cbk:i * 8 + cbk + 1]),
                       reads=[sgdb[q]], writes=[jb4], partw=[ssqpb])
                P.emit("dve", lambda e, q=q, i=i, cs=cs: e.tensor_tensor(out=m_sb3[:, i, cs], in0=sgd[q][:, 0:256], in1=gb3[:, cs], op=ALU.mult),
                       reads=[sgdb[q], gbb3], partw=[msb_bufs3[i]])
                if cbk == 7:
                    post_norm_tile(i, m_sb3, msb_bufs3, 8, last_store_row0=row0, next_load=nxt)

    for pi in range(n_pre_pass):
        prefix_pass(pi + (2 - n_pre_pass), last=(pi == n_pre_pass - 1))
    for pi in range(n_own_pass):
        own_pass(pi, last=(pi == n_own_pass - 1))
    sems = [f"stx{i}" for i in range(4)] + (["dbg"] if dbg_out else [])
    P.final_wait("sp", sems)
    P.replay()
    return nc


def make_consts():
    s = np.arange(128)[:, None]
    t = np.arange(128)[None, :]
    same = (s // 64) == (t // 64)
    mf = (same & (s <= t)).astype(np.float32)
    mr = (same & (s > t)).astype(np.float32)
    mfr = np.concatenate([mf, mr], axis=1).astype(ml_dtypes.bfloat16)
    lm = np.where(same & (s <= t), -1.0 / 16.0, 0.0).astype(np.float32)
    lrev = np.where(same & (s > t), -1.0 / 16.0, 0.0).astype(np.float32)
    lmm = np.concatenate([lm, lrev], axis=1).astype(np.float32)
    ident = np.eye(128, dtype=np.float32).astype(ml_dtypes.bfloat16)
    return ident, mfr, lmm


def make_in_maps(inputs):
    f = lambda a: np.ascontiguousarray(np.asarray(a), dtype=np.float32)
    x = f(inputs["x"])
    p = f(inputs["p"])[0]
    col = lambda g: f(g)[0].reshape(16, 128).T
    gcol = np.ascontiguousarray(np.concatenate([col(inputs["g_pre_mix"]), col(inputs["g_pre_ffn"]), col(inputs["g_pre_ple"])], axis=1))
    gpost = np.ascontiguousarray(np.stack([f(inputs["g_post_mix"])[0], f(inputs["g_post_ffn"])[0], f(inputs["g_post_ple"])[0]], axis=0))
    cw = f(inputs["conv_w"])[0]
    convw = np.ascontiguousarray(cw.reshape(3, 8, 128).transpose(2, 1, 0).reshape(128, 24))
    waug = np.zeros((33, 512), np.float32)
    waug[0:16] = f(inputs["w_alpha_up"])[0]
    waug[32] = f(inputs["b_alpha_up"])[0]
    hg = f(inputs["gla_head_gain"])[0].reshape(1, 256)
    ident, mfr, lmm = make_consts()
    shared = {
        "w_in": f(inputs["w_in"])[0], "w_a_out": f(inputs["w_a_out"])[0], "w_b_out": f(inputs["w_b_out"])[0],
        "w_mix_out": f(inputs["w_mix_out"])[0], "w_ff_gate": f(inputs["w_ff_gate"])[0], "w_ff_up": f(inputs["w_ff_up"])[0],
        "w_ff_down": f(inputs["w_ff_down"])[0], "w_ple_gate": f(inputs["w_ple_gate"])[0], "w_ple_proj": f(inputs["w_ple_proj"])[0],
        "gcol": gcol, "gpost": gpost, "convw": convw, "waug": waug, "hg": hg, "ident": ident, "mfr": mfr, "lm": lmm,
    }
    in_maps = []
    for c in range(8):
        b, half = c // 2, c % 2
        m = dict(shared)
        m["x"] = np.ascontiguousarray(x[b, half * NTOK:(half + 1) * NTOK])
        m["xp"] = np.ascontiguousarray(x[b, 0:NTOK]) if half == 1 else np.zeros((NTOK, D), np.float32)
        m["p"] = np.ascontiguousarray(p[b, half * NTOK:(half + 1) * NTOK])
        in_maps.append(m)
    return in_maps


def kernel(**inputs):
    in_maps = make_in_maps(inputs)
    nc = build_program()
    res = run_bass_kernel_spmd(nc, in_maps, core_ids=list(range(8)))
    out = np.empty((4, 2048, D), np.float32)
    for c in range(8):
        b, half = c // 2, c % 2
        out[b, half * NTOK:(half + 1) * NTOK] = np.asarray(res.results[c]["out"], dtype=np.float32)
    return out
```
